# Optimizing a Trainium2 kernel written in Bass

```python
import math
import jax, jax.numpy as jnp
from jax import lax
import numpy as np

D_MODEL = 1024
BATCH = 2
SEQ = 16384
DEPTH = 2

MIX_WIDTH = D_MODEL
CONV_CH = MIX_WIDTH // 2
CONV_GROUPS = 8
CONV_K = 31
DIFF_WIDTH = MIX_WIDTH - CONV_CH
HEAD_DIM = 64
N_DIFF_HEADS = DIFF_WIDTH // (2 * HEAD_DIM)
D_FF = 2816
NUM_BUCKETS = 32
MAX_DISTANCE = 128
BLOCK_Q = 128
EPS = 1e-6
IN_COLS = 2 * CONV_CH + 3 * DIFF_WIDTH

kernel_name = "hybrid_conformer_conv_diffattn_macaron"


def rms_norm(x, g):
    xf = x.astype(jnp.float32)
    y = xf * lax.rsqrt(jnp.mean(xf * xf, axis=-1, keepdims=True) + EPS)
    return (y * g.astype(jnp.float32)).astype(x.dtype)


def swiglu(h, w_gate, w_up, w_down):
    return (jax.nn.silu(h @ w_gate) * (h @ w_up)) @ w_down


def lambda_init_fn(layer_idx):
    return 0.8 - 0.6 * math.exp(-0.3 * layer_idx)


def rel_bucket(n):
    max_exact = NUM_BUCKETS // 2
    nf = jnp.maximum(n, 1).astype(jnp.float32)
    large = max_exact + (jnp.log(nf / max_exact) / math.log(MAX_DISTANCE / max_exact)
                         * (NUM_BUCKETS - max_exact)).astype(jnp.int32)
    large = jnp.minimum(large, NUM_BUCKETS - 1)
    return jnp.where(n < max_exact, n, large)


def conv_module(a, g, conv_w, conv_b, conv_norm):
    u = a * jax.nn.sigmoid(g)
    rhs = conv_w[:, None, :].astype(u.dtype)
    y = lax.conv_general_dilated(u, rhs, window_strides=(1,), padding=[(CONV_K - 1, 0)],
                                 dimension_numbers=("NWC", "WIO", "NWC"),
                                 feature_group_count=CONV_CH)
    y = y + conv_b
    return jax.nn.silu(rms_norm(y, conv_norm))


def diff_attention(q, k, v, rel_bias, lam, lam_init, subln_g):
    B, H, _, S, d = q.shape
    nb = S // BLOCK_Q
    scale = 1.0 / math.sqrt(d)
    qb = q.reshape(B, H, 2, nb, BLOCK_Q, d).transpose(3, 0, 1, 2, 4, 5)
    k_pos = jnp.arange(S, dtype=jnp.int32)
    bias_tab = rel_bias.astype(jnp.float32)

    def block(args):
        qi, i = args
        q_pos = i * BLOCK_Q + jnp.arange(BLOCK_Q, dtype=jnp.int32)
        dist = q_pos[:, None] - k_pos[None, :]
        bias = bias_tab[rel_bucket(jnp.maximum(dist, 0))].transpose(2, 0, 1)
        s = jnp.einsum("bhmqd,bhmkd->bhmqk", qi, k).astype(jnp.float32) * scale
        s = s + bias[None, :, None]
        s = jnp.where((dist >= 0)[None, None, None], s, -1e30)
        p = jax.nn.softmax(s, axis=-1)
        a = p[:, :, 0] - lam * p[:, :, 1]
        return jnp.einsum("bhqk,bhkd->bhqd", a.astype(v.dtype), v)

    o = lax.map(block, (qb, jnp.arange(nb, dtype=jnp.int32)))
    o = o.transpose(1, 0, 3, 2, 4).reshape(B, S, H, 2 * d)
    o = rms_norm(o, subln_g) * (1.0 - lam_init)
    return o.reshape(B, S, H * 2 * d)


def setup_inputs(seed: int = 0) -> dict:
    key = jax.random.key(seed)
    ks = jax.random.split(key, 24)
    f32 = jnp.float32

    def nrm(k, shape, scale):
        return jax.random.normal(k, shape, f32) * scale

    def gain(k, shape):
        return 1.0 + 0.02 * jax.random.normal(k, shape, f32)

    L = DEPTH
    return {
        "x": jax.random.normal(ks[0], (BATCH, SEQ, D_MODEL), f32),
        "rel_bias": nrm(ks[1], (NUM_BUCKETS, N_DIFF_HEADS), 0.5),
        "ffn1_norm": gain(ks[2], (L, D_MODEL)),
        "ffn1_w_gate": nrm(ks[3], (L, D_MODEL, D_FF), D_MODEL ** -0.5),
        "ffn1_w_up": nrm(ks[4], (L, D_MODEL, D_FF), D_MODEL ** -0.5),
        "ffn1_w_down": nrm(ks[5], (L, D_FF, D_MODEL), D_FF ** -0.5),
        "mix_norm": gain(ks[6], (L, D_MODEL)),
        "w_in": nrm(ks[7], (L, D_MODEL, IN_COLS), D_MODEL ** -0.5),
        "conv_w": nrm(ks[8], (L, CONV_K, CONV_CH), CONV_K ** -0.5),
        "conv_b": nrm(ks[9], (L, CONV_CH), 0.02),
        "conv_norm": gain(ks[10], (L, CONV_CH)),
        "q_norm": gain(ks[11], (L, HEAD_DIM)),
        "k_norm": gain(ks[12], (L, HEAD_DIM)),
        "lambda_q1": nrm(ks[13], (L, HEAD_DIM), 0.1),
        "lambda_k1": nrm(ks[14], (L, HEAD_DIM), 0.1),
        "lambda_q2": nrm(ks[15], (L, HEAD_DIM), 0.1),
        "lambda_k2": nrm(ks[16], (L, HEAD_DIM), 0.1),
        "subln_norm": gain(ks[17], (L, 2 * HEAD_DIM)),
        "w_out": nrm(ks[18], (L, MIX_WIDTH, D_MODEL), MIX_WIDTH ** -0.5),
        "ffn2_norm": gain(ks[19], (L, D_MODEL)),
        "ffn2_w_gate": nrm(ks[20], (L, D_MODEL, D_FF), D_MODEL ** -0.5),
        "ffn2_w_up": nrm(ks[21], (L, D_MODEL, D_FF), D_MODEL ** -0.5),
        "ffn2_w_down": nrm(ks[22], (L, D_FF, D_MODEL), D_FF ** -0.5),
    }


def reference(x, rel_bias, ffn1_norm, ffn1_w_gate, ffn1_w_up, ffn1_w_down, mix_norm, w_in,
              conv_w, conv_b, conv_norm, q_norm, k_norm, lambda_q1, lambda_k1, lambda_q2,
              lambda_k2, subln_norm, w_out, ffn2_norm, ffn2_w_gate, ffn2_w_up, ffn2_w_down):
    B, S, _ = x.shape
    H, d = N_DIFF_HEADS, HEAD_DIM
    splits = np.cumsum([CONV_CH, CONV_CH, DIFF_WIDTH, DIFF_WIDTH]).tolist()
    for l in range(DEPTH):
        x = x + 0.5 * swiglu(rms_norm(x, ffn1_norm[l]), ffn1_w_gate[l], ffn1_w_up[l], ffn1_w_down[l])

        h = rms_norm(x, mix_norm[l])
        a, g, q, k, v = jnp.split(h @ w_in[l], splits, axis=-1)

        conv_out = conv_module(a, g, conv_w[l], conv_b[l], conv_norm[l])

        q = q.reshape(B, S, H, 2, d).transpose(0, 2, 3, 1, 4)
        k = k.reshape(B, S, H, 2, d).transpose(0, 2, 3, 1, 4)
        q = rms_norm(q, q_norm[l])
        k = rms_norm(k, k_norm[l])
        v = v.reshape(B, S, H, 2 * d).transpose(0, 2, 1, 3)
        lam_init = lambda_init_fn(l)
        lam = (jnp.exp(jnp.sum(lambda_q1[l].astype(jnp.float32) * lambda_k1[l].astype(jnp.float32)))
               - jnp.exp(jnp.sum(lambda_q2[l].astype(jnp.float32) * lambda_k2[l].astype(jnp.float32)))
               + lam_init)
        diff_out = diff_attention(q, k, v, rel_bias, lam, lam_init, subln_norm[l])

        x = x + jnp.concatenate([conv_out, diff_out], axis=-1) @ w_out[l]

        x = x + 0.5 * swiglu(rms_norm(x, ffn2_norm[l]), ffn2_w_gate[l], ffn2_w_up[l], ffn2_w_down[l])
    return x
```

```python
import math
import numpy as np
import ml_dtypes
import concourse.bass as bass
import concourse.mybir as mybir
from concourse.bass_utils import run_bass_kernel_spmd

F32 = mybir.dt.float32
BF16 = mybir.dt.bfloat16
ALU = mybir.AluOpType
AF = mybir.ActivationFunctionType

D = 1024
B = 2
S = 16384
DEPTH = 2
NCORES = 8
TOK = 4096
NG = 8
G = 512
KC = 8
DFF = 2816
FC = 22
CONVK = 31
NH = 4
EPS = 1e-6
NV = 2688
NEG = -1e30


class Tile:
    __slots__ = ("name", "w", "r")

    def __init__(self, name=""):
        self.name = name
        self.w = None
        self.r = {}


class DSem:
    def __init__(self, sch, name):
        self.h = sch.new_sem(name)
        self.val = 0
        self.name = name


class Eng:
    def __init__(self, sch, name, h):
        self.name = name
        self.h = h
        self.sem = sch.new_sem("e_" + name)
        self.tick = 0
        self.seen = {}
        self.pending = False
        self.prog = []


class Sched:
    def __init__(self, nc, stack):
        self.nc = nc
        self.stack = stack
        self.nsem = 0
        self.strict_same_engine = False

    def new_sem(self, name):
        self.nsem += 1
        return self.stack.enter_context(self.nc.semaphore(name))

    def set_engines(self, **hs):
        self.eng = {k: Eng(self, k, h) for k, h in hs.items()}
        for k, e in self.eng.items():
            setattr(self, k, e)

    def _wait(self, eng, src, tick, force=False):
        if src is eng and not (self.strict_same_engine or force):
            return
        if eng.seen.get(src, 0) >= tick:
            return
        sh = src.h if isinstance(src, DSem) else src.sem
        eng.prog.append(lambda h, sh=sh, tick=tick: h.wait_ge(sh, tick))
        eng.seen[src] = tick

    def _deps(self, eng, reads, writes, force=False, own=None):
        deps = {}
        for t in reads:
            if t.w is not None:
                s, k = t.w
                deps[s] = max(deps.get(s, 0), k)
        for t in writes:
            if t.w is not None:
                s, k = t.w
                if s is not own:
                    deps[s] = max(deps.get(s, 0), k)
            for s, k in t.r.items():
                deps[s] = max(deps.get(s, 0), k)
        for s, k in deps.items():
            self._wait(eng, s, k, force)

    def op(self, eng, fn, reads=(), writes=(), signal=True):
        self._deps(eng, reads, writes)
        if signal:
            eng.tick += 1
            tk = eng.tick
            eng.pending = False
            eng.prog.append(lambda h, fn=fn, sem=eng.sem: fn(h).then_inc(sem, 1))
        else:
            tk = eng.tick + 1
            eng.pending = True
            eng.prog.append(fn)
        for t in reads:
            t.r[eng] = max(t.r.get(eng, 0), tk)
        for t in writes:
            t.w = (eng, tk)
            t.r = {}

    def dma(self, q, dsem, fn, reads=(), writes=(), inc=16):
        self._deps(q, reads, writes, force=True, own=dsem)
        dsem.val += inc
        q.prog.append(lambda h, fn=fn, sh=dsem.h, inc=inc: fn(h).then_inc(sh, inc))
        for t in reads:
            t.r[dsem] = max(t.r.get(dsem, 0), dsem.val)
        for t in writes:
            t.w = (dsem, dsem.val)
            t.r = {}

    def wait_tile(self, eng, t, for_write=False):
        self._deps(eng, [] if for_write else [t], [t] if for_write else [])

    def barrier(self, extra_dsems=()):
        engs = list(self.eng.values())
        for e in engs:
            assert not e.pending, e.name
        for e in engs:
            for o in engs:
                if o is not e and o.tick > 0:
                    self._wait(e, o, o.tick)
            for d in extra_dsems:
                if d.val > 0:
                    self._wait(e, d, d.val)

    def replay(self):
        nc = self.nc
        with nc.Block() as block:
            for name, e in self.eng.items():
                def body(h, e=e):
                    for f in e.prog:
                        f(h)
                getattr(block, name)(body)


def _pp_layout():
    off = {}
    cur = 0

    def add(name, n):
        nonlocal cur
        off[name] = (cur, n)
        cur += n

    for l in range(DEPTH):
        add(f"g1_{l}", KC)
        add(f"gm_{l}", KC)
        add(f"g2_{l}", KC)
        add(f"cw_{l}", 4 * CONVK)
        add(f"cb_{l}", 4)
        add(f"cn_{l}", 4)
        add(f"gq_{l}", 1)
        add(f"gk_{l}", 1)
        add(f"lq1_{l}", 64)
        add(f"lk1_{l}", 64)
        add(f"lq2_{l}", 64)
        add(f"lk2_{l}", 64)
        add(f"sub_{l}", 128)
    add("b31", NH)
    add("rb", NH)
    add("coef", 5)
    return off, cur


PP_OFF, PP_N = _pp_layout()


def lambda_init_fn(layer_idx):
    return 0.8 - 0.6 * math.exp(-0.3 * layer_idx)


def build_program(stop_after=None, debug=False):
    import contextlib
    nc = bass.Bass("TRN2", target_bir_lowering=False)

    def din(name, shape, dt=F32):
        return nc.dram_tensor(name, list(shape), dt, kind="ExternalInput").ap()

    xT_in = din("xT", [D, TOK])
    w_gate = [din("ffn1_w_gate", [DEPTH, D, DFF]), din("ffn2_w_gate", [DEPTH, D, DFF])]
    w_up = [din("ffn1_w_up", [DEPTH, D, DFF]), din("ffn2_w_up", [DEPTH, D, DFF])]
    w_down = [din("ffn1_w_down", [DEPTH, DFF, D]), din("ffn2_w_down", [DEPTH, DFF, D])]
    w_in = din("w_in", [DEPTH, D, 2560])
    w_out = din("w_out", [DEPTH, D, D])
    pp_in = din("pp", [128, PP_N])
    oh_in = din("oh", [32, NV])
    valid_in = din("valid", [NH, NV])
    yT = nc.dram_tensor("yT", [D, TOK], F32, kind="ExternalOutput").ap()

    def dscr(name, shape, dt):
        if debug and name in ("u_dram", "q_dram", "dbg_k", "dbg_v", "dbg_t", "dbg_acc", "dbg_dT", "dbg_E", "dbg_pt"):
            return nc.dram_tensor(name, list(shape), dt, kind="ExternalOutput").ap()
        return nc.dram_tensor(name, list(shape), dt).ap()

    wgu = dscr("wgu", [DEPTH, 2, FC, 128, 2, KC, 128], BF16)
    wdn = dscr("wdn", [DEPTH, 2, KC, 128, FC, 128], BF16)
    win = dscr("win", [DEPTH, 20, 128, KC, 128], BF16)
    wob = dscr("wob", [DEPTH, D, D], BF16)
    u_dram = dscr("u_dram", [4, 128, TOK], F32)
    q_dram = dscr("q_dram", [NH, 128, TOK], BF16)
    agin_k = [dscr(f"agin_k{h}", [128, TOK], BF16) for h in range(NH)]
    agout_k = [dscr(f"agout_k{h}", [4 * 128, TOK], BF16) for h in range(NH)]
    agin_v = [[dscr(f"agin_v{h}_{hf}", [128, 16 * 129], BF16) for hf in range(2)] for h in range(NH)]
    agout_v = [[dscr(f"agout_v{h}_{hf}", [4 * 128, 16 * 129], BF16) for hf in range(2)] for h in range(NH)]
    agin_t = dscr("agin_t", [4 * 128, NG * 32], F32)
    agout_t = dscr("agout_t", [4 * 4 * 128, NG * 32], F32)
    ebv_dram = dscr("ebv_dram", [NH, NV], BF16)

    with contextlib.ExitStack() as st:
        sch = Sched(nc, st)
        sch.set_engines(sync=nc.sync, tensor=nc.tensor, vector=nc.vector, scalar=nc.scalar, gpsimd=nc.gpsimd)
        SP, PE, DVE, ACT, POOL = sch.sync, sch.tensor, sch.vector, sch.scalar, sch.gpsimd

        def sb(name, shape, dt):
            return st.enter_context(nc.sbuf_tensor(name, list(shape), dt))

        def psum(name, shape, dt):
            return st.enter_context(nc.psum_tensor(name, list(shape), dt))

        NB = 28 * 1024
        NF = 4096
        xT = sb("xT_sb", [128, KC, TOK], F32)
        arb = sb("arb", [128, NB], BF16)
        arf = sb("arf", [128, NF], F32)
        pp = sb("pp_sb", [128, PP_N], F32)
        identf = sb("identf", [128, 128], F32)
        ones_bf = sb("ones_bf", [128, 128], BF16)
        blk_bf = sb("blk_bf", [128, 128], BF16)
        J_bf = sb("J_bf", [128, 128], BF16)
        dyn = sb("dyn", [128, 160], F32)
        t_const = Tile("const")
        t_x = [Tile(f"x{m}") for m in range(NG)]

        psS = [psum("psS0", [128, 1024], F32), psum("psS1", [128, 1024], F32)]
        t_S = [Tile("S0"), Tile("S1")]
        accb = [psum(f"acc{i}", [128, 512], F32) for i in range(3)]
        t_acc = [Tile(f"acc{i}") for i in range(3)]
        pmisc = psum("pmisc", [128, 512], F32)
        t_misc = Tile("misc")

        class Arena:
            def __init__(self, ap, n):
                self.ap, self.n, self.cur = ap, n, 0

            def reset(self):
                self.cur = 0

            def get(self, n, shape=None):
                n_al = (n + 15) // 16 * 16
                assert self.cur + n_al <= self.n, (self.cur, n_al, self.n)
                v = self.ap[:, self.cur:self.cur + n]
                self.cur += n_al
                return v

        AB = Arena(arb, NB)
        AF_ = Arena(arf, NF)

        def PPs(name, lo=0, n=None):
            o, sz = PP_OFF[name]
            if n is None:
                n = sz - lo
            return pp[:, o + lo:o + lo + n]

        def MM(out, lhsT, rhs, start, stop, reads, writes, sig=False, skip=False):
            sch.op(PE, lambda h: h.matmul(out, lhsT=lhsT, rhs=rhs, start=start, stop=stop, skip_group_check=skip),
                   reads, writes, signal=sig)

        def ACTF(out, in_, func, reads, writes, bias=0.0, scale=1.0, eng=None):
            sch.op(ACT, lambda h: h.activation(out=out, in_=in_, func=func, bias=bias, scale=scale), reads, writes)

        def TS(eng, out, in0, s1, s2, op0, op1, reads, writes):
            if op1 is None:
                sch.op(eng, lambda h: h.tensor_scalar(out=out, in0=in0, scalar1=s1, scalar2=None, op0=op0), reads, writes)
            else:
                sch.op(eng, lambda h: h.tensor_scalar(out=out, in0=in0, scalar1=s1, scalar2=s2, op0=op0, op1=op1), reads, writes)

        def STT(eng, out, in0, scalar, in1, op0, op1, reads, writes):
            sch.op(eng, lambda h: h.scalar_tensor_tensor(out=out, in0=in0, scalar=scalar, in1=in1, op0=op0, op1=op1), reads, writes)

        def TT(eng, out, in0, in1, op, reads, writes):
            sch.op(eng, lambda h: h.tensor_tensor(out=out, in0=in0, in1=in1, op=op), reads, writes)

        def CP(eng, out, in_, reads, writes):
            if eng is ACT:
                sch.op(eng, lambda h: h.copy(out=out, in_=in_), reads, writes)
            else:
                sch.op(eng, lambda h: h.tensor_copy(out=out, in_=in_), reads, writes)

        def RECIP(out, in_, reads, writes):
            sch.op(DVE, lambda h: h.reciprocal(out=out, in_=in_), reads, writes)

        def DMA(q, dsem, out, in_, reads, writes):
            sch.dma(q, dsem, lambda h: h.dma_start(out=out, in_=in_), reads, writes)

        ld0 = DSem(sch, "ld0")
        DMA(SP, ld0, pp[:, :], pp_in[:, :], [], [t_const])
        sch.op(POOL, lambda h: h.memset(identf[:, :], 0.0), [], [t_const])
        sch.op(POOL, lambda h: h.affine_select(out=identf[:, :], in_=identf[:, :], pattern=[[-1, 128]], compare_op=ALU.not_equal,
                                               fill=1.0, base=0, channel_multiplier=1), [], [t_const])
        jf = xT[:, 7, 0:128]
        sch.op(POOL, lambda h: h.memset(jf, 0.0), [], [t_const])
        sch.op(POOL, lambda h: h.affine_select(out=jf, in_=jf, pattern=[[1, 128]], compare_op=ALU.not_equal,
                                               fill=1.0, base=-127, channel_multiplier=1), [], [t_const])
        sch.op(POOL, lambda h: h.tensor_copy(out=J_bf[:, :], in_=jf), [], [t_const])
        sch.op(POOL, lambda h: h.memset(ones_bf[:, :], 1.0), [], [t_const])
        sch.op(POOL, lambda h: h.memset(blk_bf[:, :], 0.0), [], [t_const])
        sch.op(POOL, lambda h: h.memset(blk_bf[0:64, 0:64], 1.0), [], [t_const])
        sch.op(POOL, lambda h: h.memset(blk_bf[64:128, 64:128], 1.0), [], [t_const])
        EPSC = dyn[:, 150:151]
        sch.op(POOL, lambda h: h.memset(dyn[:, :], 0.0), [], [t_const])
        sch.op(POOL, lambda h: h.memset(EPSC, EPS), [], [t_const])
        gsub = sb("gsub", [128, DEPTH, 128], F32)
        sch.strict_same_engine = True
        for l in range(DEPTH):
            b = 16 * l
            li = lambda_init_fn(l)
            TS(DVE, dyn[:, b:b + 1], PPs(f"gq_{l}"), 0.125, None, ALU.mult, None, [t_const], [t_const])
            tmp = xT[:, 7, 256:320]
            TT(DVE, tmp, PPs(f"lq1_{l}"), PPs(f"lk1_{l}"), ALU.mult, [t_const], [t_const])
            sch.op(DVE, lambda h, b=b, tmp=tmp: h.reduce_sum(out=dyn[:, b + 2:b + 3], in_=tmp, axis=mybir.AxisListType.X), [t_const], [t_const])
            tmp2 = xT[:, 7, 320:384]
            TT(DVE, tmp2, PPs(f"lq2_{l}"), PPs(f"lk2_{l}"), ALU.mult, [t_const], [t_const])
            sch.op(DVE, lambda h, b=b, tmp2=tmp2: h.reduce_sum(out=dyn[:, b + 3:b + 4], in_=tmp2, axis=mybir.AxisListType.X), [t_const], [t_const])
            ACTF(dyn[:, b + 4:b + 6], dyn[:, b + 2:b + 4], AF.Exp, [t_const], [t_const])
            TT(DVE, dyn[:, b + 1:b + 2], dyn[:, b + 4:b + 5], dyn[:, b + 5:b + 6], ALU.subtract, [t_const], [t_const])
            TS(DVE, dyn[:, b + 1:b + 2], dyn[:, b + 1:b + 2], li, None, ALU.add, None, [t_const], [t_const])
            TS(DVE, gsub[:, l, :], PPs(f"sub_{l}"), 1.0 - li, None, ALU.mult, None, [t_const], [t_const])

        sch.strict_same_engine = False
        ohs = xT[0:32, 6, 0:NV]
        vls = xT[0:NH, 5, 0:NV]
        ebf = xT[0:NH, 4, 0:NV]
        ebb = arb[0:NH, 0:NV]
        t_oh = Tile("oh")
        DMA(SP, ld0, ohs, oh_in[:, :], [], [t_oh])
        DMA(SP, ld0, vls, valid_in[:, :], [], [t_oh])
        rb = PPs("rb")
        for c0 in range(0, NV, 512):
            n = min(512, NV - c0)
            MM(pmisc[0:NH, 0:n], rb[0:32, :], ohs[:, c0:c0 + n], True, True, [t_const, t_oh], [t_misc], sig=True)
            ACTF(ebf[:, c0:c0 + n], pmisc[0:NH, 0:n], AF.Exp, [t_misc], [t_oh])
        TT(DVE, ebb, ebf, vls, ALU.mult, [t_oh], [t_oh])
        t_ebv = Tile("ebv")
        DMA(SP, ld0, ebv_dram[:, :], ebb, [t_oh], [t_ebv])
        sch.barrier([ld0])

        cv_ring = []
        AB.reset()
        for i in range(6):
            cv_ring.append(dict(stg=xT[:, i, :], cv=AB.get(4096), t_s=Tile(f"stg{i}"), t_c=Tile(f"cv{i}"),
                                dl=DSem(sch, f"cvl{i}"), ds=DSem(sch, f"cvs{i}")))
        t_w = Tile("wscratch")
        cvn = [0]
        cast_engs = [DVE, ACT, POOL, DVE, ACT]

        def conv_piece(src_ap, stg_view_fn, cast_in_fn, cast_out_fn, dst_ap, cv_view_fn):
            s = cv_ring[cvn[0] % len(cv_ring)]
            eng = cast_engs[cvn[0] % len(cast_engs)]
            cvn[0] += 1
            DMA(SP, s["dl"], stg_view_fn(s["stg"]), src_ap, [], [s["t_s"]])
            CP(eng, cast_out_fn(s["cv"]), cast_in_fn(s["stg"]), [s["t_s"]], [s["t_c"]])
            DMA(ACT, s["ds"], dst_ap, cv_view_fn(s["cv"]), [s["t_c"]], [])

        for l in range(DEPTH):
            for f in range(2):
                for si, W in enumerate((w_gate[f], w_up[f])):
                    for c0 in range(0, FC, 4):
                        nch = min(4, FC - c0)
                        ncol = nch * 128
                        conv_piece(
                            W[l, :, c0 * 128:c0 * 128 + ncol].rearrange("(k p) n -> p k n", p=128),
                            lambda s, ncol=ncol: s.rearrange("p (k n) -> p k n", k=8)[:, :, 0:ncol],
                            lambda s, ncol=ncol, nch=nch: s.rearrange("p (k n) -> p k n", k=8)[:, :, 0:ncol].rearrange("p k (c j) -> p c k j", c=nch),
                            lambda cvt, nch=nch: cvt[:, 0:nch * 1024].rearrange("p (c k j) -> p c k j", c=nch, k=8),
                            wgu[l, f, c0:c0 + nch, :, si, :, :].rearrange("c p k j -> p c (k j)"),
                            lambda cvt, nch=nch: cvt[:, 0:nch * 1024].rearrange("p (c n) -> p c n", c=nch))
                W = w_down[f]
                for c0 in range(0, FC, 4):
                    nch = min(4, FC - c0)
                    conv_piece(
                        W[l, c0 * 128:(c0 + nch) * 128, :].rearrange("(c p) n -> p c n", p=128),
                        lambda s, nch=nch: s[:, 0:nch * 1024].rearrange("p (c n) -> p c n", c=nch),
                        lambda s, nch=nch: s[:, 0:nch * 1024].rearrange("p (c d j) -> p d c j", c=nch, d=8),
                        lambda cvt, nch=nch: cvt[:, 0:nch * 1024].rearrange("p (d c j) -> p d c j", d=8, c=nch),
                        wdn[l, f, :, :, c0:c0 + nch, :].rearrange("d p c j -> p d (c j)"),
                        lambda cvt, nch=nch: cvt[:, 0:nch * 1024].rearrange("p (d n) -> p d n", d=8))
            for c0 in range(0, 20, 4):
                conv_piece(
                    w_in[l, :, c0 * 128:c0 * 128 + 512].rearrange("(k p) n -> p k n", p=128),
                    lambda s: s.rearrange("p (k n) -> p k n", k=8),
                    lambda s: s.rearrange("p (k c j) -> p c k j", k=8, c=4),
                    lambda cvt: cvt.rearrange("p (c k j) -> p c k j", c=4, k=8),
                    win[l, c0:c0 + 4, :, :, :].rearrange("c p k j -> p c (k j)"),
                    lambda cvt: cvt.rearrange("p (c n) -> p c n", c=4))
            for r0 in range(0, 8, 4):
                conv_piece(
                    w_out[l, r0 * 128:(r0 + 4) * 128, :].rearrange("(r p) n -> p r n", p=128),
                    lambda s: s.rearrange("p (r n) -> p r n", r=4),
                    lambda s: s,
                    lambda cvt: cvt,
                    wob[l, r0 * 128:(r0 + 4) * 128, :].rearrange("(r p) n -> p r n", p=128),
                    lambda cvt: cvt.rearrange("p (r n) -> p r n", r=4))
        sch.barrier([s["ds"] for s in cv_ring] + [s["dl"] for s in cv_ring])

        ldx = DSem(sch, "ldx")
        for k in range(KC):
            DMA(SP, ldx, xT[:, k, :], xT_in[k * 128:(k + 1) * 128, :], [], t_x)
        sch.barrier([ldx])

        st_sems = []

        _dsems = {}

        def new_dsem(name):
            if name not in _dsems:
                _dsems[name] = DSem(sch, name)
                st_sems.append(_dsems[name])
            return _dsems[name]

        def rms_to_hq(hq, t_hq, rstd, t_rstd, xg, tx, gain):
            ACTF(hq, xg, AF.Square, [tx], [t_hq])
            for k in range(KC):
                MM(accb[2][:, :], ones_bf[:, :], hq[:, k, :], k == 0, k == KC - 1, [t_hq, t_const], [t_acc[2]], sig=(k == KC - 1))
            ACTF(rstd, accb[2][:, :], AF.Sqrt, [t_acc[2], t_const], [t_rstd], bias=EPSC, scale=1.0 / D)
            RECIP(rstd, rstd, [t_rstd], [t_rstd])
            for k in range(KC):
                STT(DVE, hq[:, k, :], xg[:, k, :], gain[:, k:k + 1], rstd, ALU.mult, ALU.mult, [tx, t_rstd, t_const], [t_hq])

        def phase_ffn(jobs, write_out=False):
            AB.reset(); AF_.reset()
            actT = AB.get(FC * G).rearrange("p (c n) -> p c n", c=FC); t_act = Tile("act")
            hq = AB.get(KC * G).rearrange("p (k n) -> p k n", k=KC); t_hq = Tile("hq")
            gu = [dict(ap=AB.get(2 * KC * 128).rearrange("p (s k j) -> p s k j", s=2, k=KC), t=Tile(f"gu{i}"), d=new_dsem(f"gu{i}")) for i in range(3)]
            wd = [dict(ap=AB.get(FC * 128).rearrange("p (c j) -> p c j", c=FC), t=Tile(f"wd{i}"), d=new_dsem(f"wd{i}")) for i in range(2)]
            rstd = AF_.get(G); t_rstd = Tile("rstd")
            sg = [AF_.get(G), AF_.get(G)]; t_sg = [Tile("sg0"), Tile("sg1")]
            od = new_dsem("outd")
            ngu = 0
            nwd = 0
            for m in range(NG):
                xg = xT[:, :, m * G:(m + 1) * G]
                tx = t_x[m]
                for (l, f) in jobs:
                    gain = PPs(f"g{1 if f == 0 else 2}_{l}")
                    rms_to_hq(hq, t_hq, rstd, t_rstd, xg, tx, gain)
                    for c in range(FC):
                        w = gu[ngu % 3]
                        i2 = ngu % 2
                        ngu += 1
                        DMA(SP, w["d"], w["ap"].rearrange("p s k j -> p (s k j)"), wgu[l, f, c].rearrange("p s k j -> p (s k j)"), [], [w["t"]])
                        ps = psS[i2]
                        for k in range(KC):
                            MM(ps[:, 0:G], w["ap"][:, 0, k, :], hq[:, k, :], k == 0, k == KC - 1, [w["t"], t_hq], [t_S[i2]])
                        for k in range(KC):
                            MM(ps[:, G:2 * G], w["ap"][:, 1, k, :], hq[:, k, :], k == 0, k == KC - 1, [w["t"], t_hq], [t_S[i2]], sig=(k == KC - 1))
                        ACTF(sg[i2], ps[:, 0:G], AF.Silu, [t_S[i2]], [t_sg[i2]])
                        TT(DVE, actT[:, c, :], sg[i2], ps[:, G:2 * G], ALU.mult, [t_sg[i2], t_S[i2]], [t_act])
                    for d in range(KC):
                        w = wd[nwd % 2]
                        i2 = nwd % 2
                        nwd += 1
                        DMA(SP, w["d"], w["ap"].rearrange("p c j -> p (c j)"), wdn[l, f, d].rearrange("p c j -> p (c j)"), [], [w["t"]])
                        for c in range(FC):
                            MM(accb[i2][:, :], w["ap"][:, c, :], actT[:, c, :], c == 0, c == FC - 1, [w["t"], t_act], [t_acc[i2]], sig=(c == FC - 1))
                        STT(DVE, xg[:, d, :], accb[i2][:, :], 0.5, xg[:, d, :], ALU.mult, ALU.add, [t_acc[i2], tx], [tx])
                if write_out:
                    for k in range(KC):
                        DMA(ACT, od, yT[k * 128:(k + 1) * 128, m * G:(m + 1) * G], xT[:, k, m * G:(m + 1) * G], [tx], [])
            return od

        t_u = [Tile(f"u{m}") for m in range(NG)]
        t_q = [Tile(f"q{m}") for m in range(NG)]
        t_agk = [Tile(f"agk{m}") for m in range(NG)]
        t_agv = [Tile(f"agv{m}") for m in range(NG)]
        t_agt = [Tile(f"agt{m}") for m in range(NG)]
        agout_t5 = agout_t.rearrange("(r c p) (m e) -> r c p m e", r=4, p=128, e=32)

        def phase_mixin(l):
            AB.reset(); AF_.reset()
            hq = AB.get(KC * G).rearrange("p (k n) -> p k n", k=KC); t_hq = Tile("hq")
            wi = [dict(ap=AB.get(KC * 128).rearrange("p (k j) -> p k j", k=KC), t=Tile(f"wi{i}"), d=new_dsem(f"wi{i}")) for i in range(4)]
            wv = AB.get(KC * 512).rearrange("p (c k j) -> p c k j", c=4, k=KC); t_wv = Tile("wv"); dwv = new_dsem("wv")
            sqq = AB.get(G); t_sqq = Tile("sqq")
            qst = [dict(ap=AB.get(G), t=Tile(f"qst{i}"), d=new_dsem(f"qst{i}")) for i in range(2)]
            vst = [dict(ap=AB.get(NH * 130).rearrange("p (h e) -> p h e", h=NH), t=Tile(f"vst{i}"), d=new_dsem(f"vst{i}")) for i in range(2)]
            rstd = AF_.get(G); t_rstd = Tile("rstd")
            sgm = AF_.get(G); t_sgm = Tile("sgm")
            ut = [dict(ap=AF_.get(G), t=Tile(f"ut{i}"), d=new_dsem(f"ut{i}")) for i in range(2)]
            rq = AF_.get(G); t_rq = Tile("rq")
            for v in vst:
                sch.op(POOL, lambda h, v=v: h.memset(v["ap"][:, :, 128:129], 1.0), [], [v["t"]])
            DMA(SP, dwv, wv.rearrange("p c k j -> p c (k j)"), win[l, 16:20].rearrange("c p k j -> p c (k j)"), [], [t_wv])
            nwi = 0
            nu = 0
            nq = 0
            nv = 0
            gq_s = dyn[:, 16 * l:16 * l + 1]
            gk = PPs(f"gk_{l}")
            for m in range(NG):
                xg = xT[:, :, m * G:(m + 1) * G]
                tx = t_x[m]
                rms_to_hq(hq, t_hq, rstd, t_rstd, xg, tx, PPs(f"gm_{l}"))

                def proj(c, ps_ap, t_ps, sig_last=True):
                    nonlocal nwi
                    w = wi[nwi % 4]
                    nwi += 1
                    DMA(SP, w["d"], w["ap"].rearrange("p k j -> p (k j)"), win[l, c].rearrange("p k j -> p (k j)"), [], [w["t"]])
                    for k in range(KC):
                        MM(ps_ap, w["ap"][:, k, :], hq[:, k, :], k == 0, k == KC - 1, [w["t"], t_hq], [t_ps], sig=(sig_last and k == KC - 1))

                for cc in range(4):
                    i2 = cc % 2
                    proj(cc, psS[i2][:, 0:G], t_S[i2], sig_last=False)
                    proj(4 + cc, psS[i2][:, G:2 * G], t_S[i2])
                    ACTF(sgm, psS[i2][:, G:2 * G], AF.Sigmoid, [t_S[i2]], [t_sgm])
                    u = ut[nu % 2]
                    nu += 1
                    TT(DVE, u["ap"], psS[i2][:, 0:G], sgm, ALU.mult, [t_S[i2], t_sgm], [u["t"]])
                    DMA(ACT, u["d"], u_dram[cc, :, m * G:(m + 1) * G], u["ap"], [u["t"]], [t_u[m]] if cc == 3 else [])
                    DMA(ACT, u["d"], agin_t[cc * 128:(cc + 1) * 128, m * 32:(m + 1) * 32], u["ap"][:, G - 32:G], [u["t"]], [t_agt[m]] if cc == 3 else [])
                for which in range(2):
                    for hh in range(NH):
                        i2 = hh % 2
                        c = 8 + 4 * which + hh
                        proj(c, psS[i2][:, 0:G], t_S[i2])
                        ACTF(sqq, psS[i2][:, 0:G], AF.Square, [t_S[i2]], [t_sqq])
                        MM(psS[i2][:, G:2 * G], blk_bf[:, :], sqq, True, True, [t_sqq, t_const], [t_S[i2]], sig=True)
                        ACTF(rq, psS[i2][:, G:2 * G], AF.Sqrt, [t_S[i2], t_const], [t_rq], bias=EPSC, scale=1.0 / 64)
                        RECIP(rq, rq, [t_rq], [t_rq])
                        qs = qst[nq % 2]
                        nq += 1
                        STT(DVE, qs["ap"], psS[i2][:, 0:G], gq_s if which == 0 else gk, rq, ALU.mult, ALU.mult, [t_S[i2], t_rq, t_const], [qs["t"]])
                        if which == 0:
                            DMA(ACT, qs["d"], q_dram[hh, :, m * G:(m + 1) * G], qs["ap"], [qs["t"]], [t_q[m]] if hh == NH - 1 else [])
                        else:
                            DMA(ACT, qs["d"], agin_k[hh][:, m * G:(m + 1) * G], qs["ap"], [qs["t"]], [t_agk[m]] if hh == NH - 1 else [])
                for tb in range(4):
                    i2 = tb % 2
                    for k in range(KC):
                        MM(accb[i2][:, :], hq[:, k, tb * 128:(tb + 1) * 128], wv[:, :, k, :], k == 0, k == KC - 1, [t_hq, t_wv], [t_acc[i2]], sig=(k == KC - 1))
                    vs = vst[nv % 2]
                    nv += 1
                    CP(DVE, vs["ap"][:, :, 0:128], accb[i2][:, :].rearrange("p (h e) -> p h e", h=NH), [t_acc[i2]], [vs["t"]])
                    for hh in range(NH):
                        kbl = 4 * (m % 4) + tb
                        DMA(ACT, vs["d"], agin_v[hh][m // 4][:, kbl * 129:(kbl + 1) * 129], vs["ap"][:, hh, 0:129], [vs["t"]],
                            [t_agv[m]] if (tb == 3 and hh == NH - 1) else [])

        t_gk = [Tile(f"gk{h}") for h in range(NH)]; t_gv = [Tile(f"gv{h}") for h in range(NH)]; t_gt = Tile("gt")
        ccs = DSem(sch, "ccs")
        RG = [[0, 1, 2, 3], [4, 5, 6, 7]]

        def phase_gather():
            sch.barrier(st_sems)
            jobs = [(agin_t, agout_t, t_agt, t_gt)]
            for hh in range(NH):
                jobs.append((agin_k[hh], agout_k[hh], t_agk, t_gk[hh]))
                for hf in range(2):
                    jobs.append((agin_v[hh][hf], agout_v[hh][hf], t_agv, t_gv[hh]))
            for (i_ap, o_ap, tin, tout) in jobs:
                sch.dma(POOL, ccs, lambda h, i_ap=i_ap, o_ap=o_ap: h.collective_compute("AllGather", ALU.bypass, replica_groups=RG,
                                                                                      ins=[i_ap[:, :]], outs=[o_ap[:, :]]),
                        reads=list(tin), writes=[tout], inc=1)
            sch.barrier([])

        def phase_conv(l):
            AB.reset(); AF_.reset()
            woc = AB.get(4 * D).rearrange("p (r n) -> p r n", r=4); t_woc = Tile("woc"); dwoc = new_dsem("woc")
            sqc = AB.get(4 * G).rearrange("p (c n) -> p c n", c=4); t_sqc = Tile("sqc")
            co = AB.get(4 * G).rearrange("p (c n) -> p c n", c=4); t_co = Tile("co")
            dg = AB.get(4 * CONVK * 128).rearrange("p (t j) -> p t j", t=4 * CONVK); t_dg = Tile("dg")
            uwb = [dict(ap=AB.get(G + 32), t=Tile(f"uwb{i}")) for i in range(2)]
            uw = [dict(ap=AF_.get(G + 32), t=Tile(f"uw{i}"), d=new_dsem(f"uw{i}")) for i in range(2)]
            cand = [dict(ap=AF_.get(5 * 32).rearrange("p (r e) -> p r e", r=5), t=Tile(f"cand{i}"), d=new_dsem(f"cand{i}")) for i in range(2)]
            ycv = AF_.get(4 * G).rearrange("p (c n) -> p c n", c=4); t_y = Tile("ycv")
            rc = AF_.get(G); t_rc = Tile("rc")
            DMA(SP, dwoc, woc, wob[l, 0:512, :].rearrange("(r p) n -> p r n", p=128), [], [t_woc])
            coef = PPs("coef")
            cw = PPs(f"cw_{l}")
            cb = PPs(f"cb_{l}")
            cn = PPs(f"cn_{l}")
            for tj in range(4 * CONVK):
                TS(DVE, dg[:, tj, :], identf[:, :], cw[:, tj:tj + 1], None, ALU.mult, None, [t_const], [t_dg])
            n = 0
            for m in range(NG):
                xg = xT[:, :, m * G:(m + 1) * G]
                tx = t_x[m]
                for cc in range(4):
                    w = uw[n % 2]
                    wb = uwb[n % 2]
                    i2 = n % 2
                    cd = cand[n % 2]
                    n += 1
                    eng = DVE
                    DMA(SP, w["d"], w["ap"][:, 32:32 + G], u_dram[cc, :, m * G:(m + 1) * G], [t_u[m]], [w["t"]])
                    DMA(SP, cd["d"], cd["ap"][:, 0:4, :], agout_t5[:, cc, :, m, :].rearrange("r p e -> p r e"), [t_gt], [cd["t"]])
                    if m > 0:
                        DMA(SP, cd["d"], cd["ap"][:, 4, :], agout_t5[3, cc, :, m - 1, :], [t_gt], [cd["t"]])
                    halo = w["ap"][:, 0:32]
                    sch.strict_same_engine = True
                    TS(eng, halo, cd["ap"][:, 0, :], coef[:, 0:1], None, ALU.mult, None, [cd["t"], t_const], [w["t"]])
                    for r in range(1, 5 if m > 0 else 4):
                        STT(eng, halo, cd["ap"][:, r, :], coef[:, r:r + 1], halo, ALU.mult, ALU.add, [cd["t"], t_const], [w["t"]])
                    sch.strict_same_engine = False
                    CP(ACT, wb["ap"], w["ap"], [w["t"]], [wb["t"]])
                    for j in range(CONVK):
                        MM(psS[i2][:, 0:G], dg[:, cc * CONVK + j, :], wb["ap"][:, 2 + j:2 + j + G], j == 0, j == CONVK - 1, [t_dg, wb["t"]], [t_S[i2]], sig=(j == CONVK - 1))
                    TS(DVE, ycv[:, cc, :], psS[i2][:, 0:G], cb[:, cc:cc + 1], None, ALU.add, None, [t_S[i2], t_const], [t_y])
                ACTF(sqc, ycv, AF.Square, [t_y], [t_sqc])
                for cc in range(4):
                    MM(accb[2][:, :], ones_bf[:, :], sqc[:, cc, :], cc == 0, cc == 3, [t_sqc, t_const], [t_acc[2]], sig=(cc == 3))
                ACTF(rc, accb[2][:, :], AF.Sqrt, [t_acc[2], t_const], [t_rc], bias=EPSC, scale=1.0 / 512)
                RECIP(rc, rc, [t_rc], [t_rc])
                for cc in range(4):
                    STT(DVE, ycv[:, cc, :], ycv[:, cc, :], cn[:, cc:cc + 1], rc, ALU.mult, ALU.mult, [t_y, t_rc, t_const], [t_y])
                ACTF(co, ycv, AF.Silu, [t_y], [t_co])
                for d in range(KC):
                    i2 = d % 2
                    for r in range(4):
                        MM(accb[i2][:, :], woc[:, r, d * 128:(d + 1) * 128], co[:, r, :], r == 0, r == 3, [t_woc, t_co], [t_acc[i2]], sig=(r == 3))
                    TT(DVE, xg[:, d, :], accb[i2][:, :], xg[:, d, :], ALU.add, [t_acc[i2], tx], [tx])

        def phase_attn(l):
            AB.reset(); AF_.reset()
            E = AB.get(17 * G).rearrange("p (i n) -> p i n", i=17); t_E = Tile("E")
            hst = [dict(ap=AB.get(G), t=Tile(f"hst{i}"), d=new_dsem(f"hst{i}")) for i in range(2)]
            qt = [dict(ap=AB.get(G), t=Tile(f"qt{i}"), d=new_dsem(f"qt{i}")) for i in range(2)]
            kv = [dict(k=AB.get(2 * 2048).rearrange("p (s n) -> p s n", s=2), v=AB.get(16 * 129).rearrange("p (b e) -> p b e", b=16),
                       t=Tile(f"kv{i}"), d=new_dsem(f"kv{i}")) for i in range(2)]
            pt = [dict(ap=AB.get(2 * G), t=Tile(f"pt{i}")) for i in range(3)]
            diffT = AB.get(G); t_dT = Tile("diffT")
            woh = AB.get(D); t_woh = Tile("woh"); dwoh = new_dsem("woh")
            t2q = [AF_.get(128) for _ in range(4)]; t_t2q = [Tile(f"t2{i}") for i in range(4)]
            ddq = [AF_.get(128) for _ in range(4)]; t_ddq = [Tile(f"dd{i}") for i in range(4)]
            dnq = [AF_.get(128) for _ in range(4)]; t_dnq = [Tile(f"dn{i}") for i in range(4)]
            smq = [AF_.get(128) for _ in range(4)]; t_smq = [Tile(f"sm{i}") for i in range(4)]
            for s in kv:
                sch.op(POOL, lambda h, s=s: h.memset(s["k"][64:128, 0, :], 0.0), [], [s["t"]])
                sch.op(POOL, lambda h, s=s: h.memset(s["k"][0:64, 1, :], 0.0), [], [s["t"]])
            lamt = dyn[:, 16 * l + 1:16 * l + 2]
            b31 = PPs("b31")
            nkv = 0
            nqt = 0
            npt = 0
            nS = 0
            nh = 0
            for hh in range(NH):
                for i in range(17):
                    rp = i - 1
                    hs = hst[nh % 2]
                    nh += 1
                    src = bass.AP(ebv_dram.tensor, hh * NV + 1920 - 128 * rp, [[1, 128], [1, G]])
                    DMA(SP, hs["d"], hs["ap"], src, [t_ebv], [hs["t"]])
                    MM(pmisc[:, :], J_bf[:, :], hs["ap"], True, True, [hs["t"], t_const], [t_misc], sig=True)
                    CP(DVE, E[:, i, :], pmisc[:, :], [t_misc], [t_E])
                DMA(SP, dwoh, woh, wob[l, 512 + hh * 128:512 + (hh + 1) * 128, :], [], [t_woh])
                for m in range(NG):
                    xg = xT[:, :, m * G:(m + 1) * G]
                    tx = t_x[m]
                    q = qt[nqt % 2]
                    nqt += 1
                    DMA(SP, q["d"], q["ap"], q_dram[hh, :, m * G:(m + 1) * G], [t_q[m]], [q["t"]])
                    nu = 16 * (m + 1)
                    slots = {}

                    def load_sb(M):
                        nonlocal nkv
                        s = kv[nkv % 2]
                        nkv += 1
                        for r in range(4):
                            for half in range(2):
                                DMA(SP, s["d"], s["k"][half * 64:(half + 1) * 64, half, r * 512:(r + 1) * 512],
                                    agout_k[hh][r * 128 + half * 64:r * 128 + (half + 1) * 64, M * 512:(M + 1) * 512],
                                    [t_gk[hh]], [s["t"]])
                            DMA(SP, s["d"], s["v"][:, 4 * r:4 * r + 4, :].rearrange("p b e -> p (b e)"),
                                agout_v[hh][M // 4][r * 128:(r + 1) * 128, 4 * (M % 4) * 129:(4 * (M % 4) + 4) * 129], [t_gv[hh]], [s["t"]])
                        slots[M] = s

                    def unit(u):
                        M, kb = u // 16, u % 16
                        return slots[M], kb, 16 * (M - m) + kb

                    def QK(u):
                        s, kb, rp = unit(u)
                        i2 = (nS + u) % 2
                        MM(psS[i2][:, 0:G], s["k"][:, 0, kb * 128:(kb + 1) * 128], q["ap"], True, True, [s["t"], q["t"]], [t_S[i2]])
                        MM(psS[i2][:, G:2 * G], s["k"][:, 1, kb * 128:(kb + 1) * 128], q["ap"], True, True, [s["t"], q["t"]], [t_S[i2]], sig=True)

                    def EXPPV(u):
                        s, kb, rp = unit(u)
                        i2 = (nS + u) % 2
                        p = pt[(npt + u) % 3]
                        if rp >= -1:
                            ACTF(p["ap"], psS[i2][:, :], AF.Exp, [t_S[i2]], [p["t"]])
                            for mp in range(2):
                                TT(DVE, p["ap"][:, mp * G:(mp + 1) * G], p["ap"][:, mp * G:(mp + 1) * G], E[:, rp + 1, :], ALU.mult, [p["t"], t_E], [p["t"]])
                        else:
                            ACTF(p["ap"], psS[i2][:, :], AF.Exp, [t_S[i2], t_const], [p["t"]], bias=b31[:, hh:hh + 1])
                        for qb in range(4):
                            for mp in range(2):
                                a = qb * 2 + mp
                                bk, off = a // 3, (a % 3) * 130
                                MM(accb[bk][:, off:off + 129], p["ap"][:, mp * G + qb * 128:mp * G + (qb + 1) * 128], s["v"][:, kb, 0:129],
                                   (u == 0 and a % 3 == 0), (u == nu - 1), [p["t"], s["t"]], [t_acc[bk]], sig=(a == 7), skip=True)

                    load_sb(0)
                    QK(0)
                    for u in range(nu):
                        if u % 16 == 0 and u // 16 + 1 <= m:
                            load_sb(u // 16 + 1)
                        if u + 1 < nu:
                            QK(u + 1)
                        EXPPV(u)
                    nS += nu
                    npt += nu
                    if stop_after == 6:
                        dacc = dscr("dbg_acc", [128, 3 * 512], F32)
                        dE = dscr("dbg_E", [128, 17 * G], BF16)
                        dpt = dscr("dbg_pt", [128, 3 * 2 * G], BF16)
                        dd_ = new_dsem("dbgd")
                        for bk in range(3):
                            stg = xT[:, 7, bk * 512:(bk + 1) * 512]
                            CP(DVE, stg, accb[bk][:, :], [t_acc[bk]], [t_x[7]])
                            DMA(SP, dd_, dacc[:, bk * 512:(bk + 1) * 512], stg, [t_x[7]], [])
                        DMA(SP, dd_, dE[:, :], E.rearrange("p i n -> p (i n)"), [t_E], [])
                        for i3 in range(3):
                            DMA(SP, dd_, dpt[:, i3 * 2 * G:(i3 + 1) * 2 * G], pt[i3]["ap"], [pt[i3]["t"]], [])
                    sch.strict_same_engine = True
                    QB = range(4)
                    A1 = [accb[(2 * qb) // 3][:, ((2 * qb) % 3) * 130:((2 * qb) % 3) * 130 + 129] for qb in QB]
                    A2 = [accb[(2 * qb + 1) // 3][:, ((2 * qb + 1) % 3) * 130:((2 * qb + 1) % 3) * 130 + 129] for qb in QB]
                    tA1 = [t_acc[(2 * qb) // 3] for qb in QB]
                    tA2 = [t_acc[(2 * qb + 1) // 3] for qb in QB]
                    for qb in QB:
                        RECIP(smq[qb][:, 0:1], A1[qb][:, 128:129], [tA1[qb]], [t_smq[qb]])
                    for qb in QB:
                        RECIP(smq[qb][:, 1:2], A2[qb][:, 128:129], [tA2[qb]], [t_smq[qb]])
                    for qb in QB:
                        TS(DVE, smq[qb][:, 1:2], smq[qb][:, 1:2], lamt, None, ALU.mult, None, [t_smq[qb], t_const], [t_smq[qb]])
                    for qb in QB:
                        TS(DVE, t2q[qb], A2[qb][:, 0:128], smq[qb][:, 1:2], None, ALU.mult, None, [tA2[qb], t_smq[qb]], [t_t2q[qb]])
                    for qb in QB:
                        STT(DVE, ddq[qb], A1[qb][:, 0:128], smq[qb][:, 0:1], t2q[qb], ALU.mult, ALU.subtract, [tA1[qb], t_smq[qb], t_t2q[qb]], [t_ddq[qb]])
                    for qb in QB:
                        TT(DVE, t2q[qb], ddq[qb], ddq[qb], ALU.mult, [t_ddq[qb]], [t_t2q[qb]])
                    for qb in QB:
                        sch.op(DVE, lambda h, qb=qb: h.reduce_sum(out=smq[qb][:, 2:3], in_=t2q[qb], axis=mybir.AxisListType.X), [t_t2q[qb]], [t_smq[qb]])
                    for qb in QB:
                        ACTF(smq[qb][:, 3:4], smq[qb][:, 2:3], AF.Sqrt, [t_smq[qb], t_const], [t_smq[qb]], bias=EPSC, scale=1.0 / 128)
                    for qb in QB:
                        RECIP(smq[qb][:, 3:4], smq[qb][:, 3:4], [t_smq[qb]], [t_smq[qb]])
                    for qb in QB:
                        STT(DVE, dnq[qb], ddq[qb], smq[qb][:, 3:4], gsub[:, l, :], ALU.mult, ALU.mult, [t_ddq[qb], t_smq[qb], t_const], [t_dnq[qb]])
                    for hf in range(2):
                        for q2 in range(2):
                            qb = 2 * hf + q2
                            sch.op(PE, lambda h, qb=qb, q2=q2: h.transpose(pmisc[:, q2 * 256:q2 * 256 + 128], dnq[qb], identf[:, :]), [t_dnq[qb], t_const], [t_misc])
                        CP(ACT, diffT[:, hf * 256:(hf + 1) * 256].rearrange("p (a b) -> p a b", a=2),
                           pmisc[:, :].rearrange("p (a b) -> p a b", a=2)[:, :, 0:128], [t_misc], [t_dT])
                    sch.strict_same_engine = False
                    for d in range(KC):
                        i2 = d % 2
                        MM(psS[i2][:, 0:G], woh[:, d * 128:(d + 1) * 128], diffT, True, True, [t_woh, t_dT], [t_S[i2]], sig=True)
                        TT(DVE, xg[:, d, :], psS[i2][:, 0:G], xg[:, d, :], ALU.add, [t_S[i2], tx], [tx])
                    if stop_after == 6:
                        ddT = dscr("dbg_dT", [128, G], BF16)
                        DMA(SP, new_dsem("dbgd"), ddT[:, :], diffT, [t_dT], [])
                        return

        def dump_x():
            od = new_dsem("outd")
            for m in range(NG):
                for k in range(KC):
                    DMA(ACT, od, yT[k * 128:(k + 1) * 128, m * G:(m + 1) * G], xT[:, k, m * G:(m + 1) * G], [t_x[m]], [])

        def run_all():
            if stop_after == 0:
                return False
            for l in range(DEPTH):
                jobs = [(l, 0)] if l == 0 else [(l - 1, 1), (l, 0)]
                phase_ffn(jobs)
                sch.barrier(st_sems)
                if stop_after == 1:
                    return False
                phase_mixin(l)
                if stop_after == 2:
                    sch.barrier(st_sems)
                    return False
                phase_gather()
                if debug:
                    dk = dscr("dbg_k", [4 * 128, TOK], BF16)
                    dv = dscr("dbg_v", [4 * 128, 16 * 129], BF16)
                    dt_ = dscr("dbg_t", [4 * 4 * 128, NG * 32], F32)
                    dd_ = new_dsem("dbgd")
                    DMA(SP, dd_, dk[:, :], agout_k[1][:, :], [t_gk[1]], [])
                    DMA(SP, dd_, dv[:, :], agout_v[1][1][:, :], [t_gv[1]], [])
                    DMA(SP, dd_, dt_[:, :], agout_t[:, :], [t_gt], [])
                if stop_after == 3:
                    sch.barrier(st_sems)
                    return False
                phase_conv(l)
                sch.barrier(st_sems)
                if stop_after == 4:
                    return False
                phase_attn(l)
                sch.barrier(st_sems)
                if stop_after in (5, 6):
                    return False
            return True

        if run_all():
            phase_ffn([(DEPTH - 1, 1)], write_out=True)
        else:
            dump_x()
        sch.barrier(st_sems + [ccs])
        sch.replay()
    return nc


def _rel_bucket_np(n):
    n = np.asarray(n, dtype=np.int64)
    nf = np.maximum(n, 1).astype(np.float64)
    large = 16 + np.floor(np.log(nf / 16.0) / math.log(128 / 16) * 16 + 1e-9).astype(np.int64)
    large = np.minimum(large, 31)
    return np.where(n < 16, n, large)


def _core_consts(j):
    npr = np.arange(NV)
    dist = npr + 512 * j - 2047
    ok = dist >= 0
    bk = _rel_bucket_np(np.maximum(dist, 0))
    oh = np.zeros((32, NV), np.float32)
    oh[bk[ok], npr[ok]] = 1.0
    valid = np.broadcast_to(ok.astype(np.float32)[None, :], (NH, NV)).copy()
    coef = np.zeros(5, np.float32)
    if j >= 1:
        coef[j - 1] = 1.0
    else:
        coef[4] = 1.0
    return oh, valid, coef


def _pack_params(inp, coef):
    pp = np.zeros((128, PP_N), np.float32)

    def put(name, arr):
        o, n = PP_OFF[name]
        arr = np.asarray(arr, np.float32)
        assert arr.shape == (128, n), (name, arr.shape, n)
        pp[:, o:o + n] = arr

    f = lambda a: np.asarray(a, np.float32)
    for l in range(DEPTH):
        put(f"g1_{l}", f(inp["ffn1_norm"][l]).reshape(KC, 128).T)
        put(f"gm_{l}", f(inp["mix_norm"][l]).reshape(KC, 128).T)
        put(f"g2_{l}", f(inp["ffn2_norm"][l]).reshape(KC, 128).T)
        cw = f(inp["conv_w"][l])
        put(f"cw_{l}", cw.reshape(CONVK, 4, 128).transpose(2, 1, 0).reshape(128, 4 * CONVK))
        put(f"cb_{l}", f(inp["conv_b"][l]).reshape(4, 128).T)
        put(f"cn_{l}", f(inp["conv_norm"][l]).reshape(4, 128).T)
        put(f"gq_{l}", np.tile(f(inp["q_norm"][l]), 2)[:, None])
        put(f"gk_{l}", np.tile(f(inp["k_norm"][l]), 2)[:, None])
        for nm, key in (("lq1", "lambda_q1"), ("lk1", "lambda_k1"), ("lq2", "lambda_q2"), ("lk2", "lambda_k2")):
            put(f"{nm}_{l}", np.broadcast_to(f(inp[key][l])[None, :], (128, 64)))
        put(f"sub_{l}", np.broadcast_to(f(inp["subln_norm"][l])[None, :], (128, 128)))
    rbias = f(inp["rel_bias"])
    put("b31", np.broadcast_to(rbias[31][None, :], (128, NH)))
    rbp = np.zeros((128, NH), np.float32)
    rbp[0:32] = rbias
    put("rb", rbp)
    put("coef", np.broadcast_to(coef[None, :], (128, 5)))
    return pp


_NC_CACHE = {}


def kernel(**inputs):
    inp = {k: np.asarray(v) for k, v in inputs.items()}
    x = np.asarray(inp["x"], np.float32)
    if "nc" not in _NC_CACHE:
        _NC_CACHE["nc"] = build_program()
    nc = _NC_CACHE["nc"]
    in_maps = []
    wnames = ["ffn1_w_gate", "ffn2_w_gate", "ffn1_w_up", "ffn2_w_up", "ffn1_w_down", "ffn2_w_down", "w_in", "w_out"]
    wts = {k: np.ascontiguousarray(inp[k], dtype=np.float32) for k in wnames}
    for c in range(NCORES):
        b, j = c // 4, c % 4
        xs = x[b].reshape(NG, 4, G, D)[:, j].reshape(TOK, D)
        oh, valid, coef = _core_consts(j)
        d = dict(wts)
        d["xT"] = np.ascontiguousarray(xs.T)
        d["pp"] = _pack_params(inp, coef)
        d["oh"] = oh
        d["valid"] = valid
        in_maps.append(d)
    res = run_bass_kernel_spmd(nc, in_maps, core_ids=list(range(NCORES)))
    out = np.empty((B, S, D), np.float32)
    for c in range(NCORES):
        b, j = c // 4, c % 4
        y = np.asarray(res.results[c]["yT"], np.float32).T.reshape(NG, G, D)
        out[b].reshape(NG, 4, G, D)[:, j] = y
    return out
```

```python
import math
import numpy as np
import ml_dtypes
import concourse.bass as bass
import concourse.mybir as mybir
from concourse.bass_utils import run_bass_kernel_spmd

F32 = mybir.dt.float32
BF16 = mybir.dt.bfloat16
ALU = mybir.AluOpType
AF = mybir.ActivationFunctionType

D = 1024
B = 2
S = 16384
DEPTH = 2
NCORES = 8
TOK = 4096
NG = 8
G = 512
KC = 8
DFF = 2816
FC = 22
CONVK = 31
NH = 4
EPS = 1e-6
NV = 2688
NEG = -1e30


class Tile:
    __slots__ = ("name", "w", "r")

    def __init__(self, name=""):
        self.name = name
        self.w = None
        self.r = {}


class DSem:
    def __init__(self, sch, name):
        self.h = sch.new_sem(name)
        self.val = 0
        self.name = name


class Eng:
    def __init__(self, sch, name, h):
        self.name = name
        self.h = h
        self.sem = sch.new_sem("e_" + name)
        self.tick = 0
        self.seen = {}
        self.pending = False
        self.prog = []


class Sched:
    def __init__(self, nc, stack):
        self.nc = nc
        self.stack = stack
        self.nsem = 0
        self.strict_same_engine = False

    def new_sem(self, name):
        self.nsem += 1
        return self.stack.enter_context(self.nc.semaphore(name))

    def set_engines(self, **hs):
        self.eng = {k: Eng(self, k, h) for k, h in hs.items()}
        for k, e in self.eng.items():
            setattr(self, k, e)

    def _wait(self, eng, src, tick, force=False):
        if src is eng and not (self.strict_same_engine or force):
            return
        if eng.seen.get(src, 0) >= tick:
            return
        sh = src.h if isinstance(src, DSem) else src.sem
        eng.prog.append(lambda h, sh=sh, tick=tick: h.wait_ge(sh, tick))
        eng.seen[src] = tick

    def _deps(self, eng, reads, writes, force=False, own=None):
        deps = {}
        for t in reads:
            if t.w is not None:
                s, k = t.w
                deps[s] = max(deps.get(s, 0), k)
        for t in writes:
            if t.w is not None:
                s, k = t.w
                if s is not own:
                    deps[s] = max(deps.get(s, 0), k)
            for s, k in t.r.items():
                deps[s] = max(deps.get(s, 0), k)
        for s, k in deps.items():
            self._wait(eng, s, k, force)

    def op(self, eng, fn, reads=(), writes=(), signal=True):
        self._deps(eng, reads, writes)
        if signal:
            eng.tick += 1
            tk = eng.tick
            eng.pending = False
            eng.prog.append(lambda h, fn=fn, sem=eng.sem: fn(h).then_inc(sem, 1))
        else:
            tk = eng.tick + 1
            eng.pending = True
            eng.prog.append(fn)
        for t in reads:
            t.r[eng] = max(t.r.get(eng, 0), tk)
        for t in writes:
            t.w = (eng, tk)
            t.r = {}

    def dma(self, q, dsem, fn, reads=(), writes=(), inc=16):
        self._deps(q, reads, writes, force=True, own=dsem)
        dsem.val += inc
        q.prog.append(lambda h, fn=fn, sh=dsem.h, inc=inc: fn(h).then_inc(sh, inc))
        for t in reads:
            t.r[dsem] = max(t.r.get(dsem, 0), dsem.val)
        for t in writes:
            t.w = (dsem, dsem.val)
            t.r = {}

    def wait_tile(self, eng, t, for_write=False):
        self._deps(eng, [] if for_write else [t], [t] if for_write else [])

    def barrier(self, extra_dsems=()):
        engs = list(self.eng.values())
        for e in engs:
            assert not e.pending, e.name
        for e in engs:
            for o in engs:
                if o is not e and o.tick > 0:
                    self._wait(e, o, o.tick)
            for d in extra_dsems:
                if d.val > 0:
                    self._wait(e, d, d.val)

    def replay(self):
        nc = self.nc
        with nc.Block() as block:
            for name, e in self.eng.items():
                def body(h, e=e):
                    for f in e.prog:
                        f(h)
                getattr(block, name)(body)


def _pp_layout():
    off = {}
    cur = 0

    def add(name, n):
        nonlocal cur
        off[name] = (cur, n)
        cur += n

    for l in range(DEPTH):
        add(f"g1_{l}", KC)
        add(f"gm_{l}", KC)
        add(f"g2_{l}", KC)
        add(f"cw_{l}", 4 * CONVK)
        add(f"cb_{l}", 4)
        add(f"cn_{l}", 4)
        add(f"gq_{l}", 1)
        add(f"gk_{l}", 1)
        add(f"lq1_{l}", 64)
        add(f"lk1_{l}", 64)
        add(f"lq2_{l}", 64)
        add(f"lk2_{l}", 64)
        add(f"sub_{l}", 128)
    add("b31", NH)
    add("rb", NH)
    add("coef", 5)
    return off, cur


PP_OFF, PP_N = _pp_layout()


def lambda_init_fn(layer_idx):
    return 0.8 - 0.6 * math.exp(-0.3 * layer_idx)


def build_program(stop_after=None, debug=False):
    import contextlib
    nc = bass.Bass("TRN2", target_bir_lowering=False)

    def din(name, shape, dt=F32):
        return nc.dram_tensor(name, list(shape), dt, kind="ExternalInput").ap()

    xT_in = din("xT", [D, TOK])
    w_gate = [din("ffn1_w_gate", [DEPTH, D, DFF]), din("ffn2_w_gate", [DEPTH, D, DFF])]
    w_up = [din("ffn1_w_up", [DEPTH, D, DFF]), din("ffn2_w_up", [DEPTH, D, DFF])]
    w_down = [din("ffn1_w_down", [DEPTH, DFF, D]), din("ffn2_w_down", [DEPTH, DFF, D])]
    w_in = din("w_in", [DEPTH, D, 2560])
    w_out = din("w_out", [DEPTH, D, D])
    pp_in = din("pp", [128, PP_N])
    oh_in = din("oh", [32, NV])
    valid_in = din("valid", [NH, NV])
    yT = nc.dram_tensor("yT", [D, TOK], F32, kind="ExternalOutput").ap()

    def dscr(name, shape, dt):
        if debug and name in ("u_dram", "q_dram", "dbg_k", "dbg_v", "dbg_t", "dbg_acc", "dbg_dT", "dbg_E", "dbg_pt"):
            return nc.dram_tensor(name, list(shape), dt, kind="ExternalOutput").ap()
        return nc.dram_tensor(name, list(shape), dt).ap()

    wgu = dscr("wgu", [DEPTH, 2, FC, 128, 2, KC, 128], BF16)
    wdn = dscr("wdn", [DEPTH, 2, KC, 128, FC, 128], BF16)
    win = dscr("win", [DEPTH, 20, 128, KC, 128], BF16)
    wob = dscr("wob", [DEPTH, D, D], BF16)
    u_dram = dscr("u_dram", [4, 128, TOK], F32)
    q_dram = dscr("q_dram", [NH, 128, TOK], BF16)
    agin_k = [dscr(f"agin_k{h}", [128, TOK], BF16) for h in range(NH)]
    agout_k = [dscr(f"agout_k{h}", [4 * 128, TOK], BF16) for h in range(NH)]
    agin_v = [[dscr(f"agin_v{h}_{hf}", [128, 16 * 129], BF16) for hf in range(2)] for h in range(NH)]
    agout_v = [[dscr(f"agout_v{h}_{hf}", [4 * 128, 16 * 129], BF16) for hf in range(2)] for h in range(NH)]
    agin_t = dscr("agin_t", [4 * 128, NG * 32], F32)
    agout_t = dscr("agout_t", [4 * 4 * 128, NG * 32], F32)
    ebv_dram = dscr("ebv_dram", [NH, NV], BF16)

    with contextlib.ExitStack() as st:
        sch = Sched(nc, st)
        sch.set_engines(sync=nc.sync, tensor=nc.tensor, vector=nc.vector, scalar=nc.scalar, gpsimd=nc.gpsimd)
        SP, PE, DVE, ACT, POOL = sch.sync, sch.tensor, sch.vector, sch.scalar, sch.gpsimd

        def sb(name, shape, dt):
            return st.enter_context(nc.sbuf_tensor(name, list(shape), dt))

        def psum(name, shape, dt):
            return st.enter_context(nc.psum_tensor(name, list(shape), dt))

        NB = 28 * 1024
        NF = 4096
        xT = sb("xT_sb", [128, KC, TOK], F32)
        arb = sb("arb", [128, NB], BF16)
        arf = sb("arf", [128, NF], F32)
        pp = sb("pp_sb", [128, PP_N], F32)
        identf = sb("identf", [128, 128], F32)
        ones_bf = sb("ones_bf", [128, 128], BF16)
        blk_bf = sb("blk_bf", [128, 128], BF16)
        J_bf = sb("J_bf", [128, 128], BF16)
        dyn = sb("dyn", [128, 160], F32)
        t_const = Tile("const")
        t_x = [Tile(f"x{m}") for m in range(NG)]

        psS = [psum("psS0", [128, 1024], F32), psum("psS1", [128, 1024], F32)]
        t_S = [Tile("S0"), Tile("S1")]
        accb = [psum(f"acc{i}", [128, 512], F32) for i in range(3)]
        t_acc = [Tile(f"acc{i}") for i in range(3)]
        pmisc = psum("pmisc", [128, 512], F32)
        t_misc = Tile("misc")

        class Arena:
            def __init__(self, ap, n):
                self.ap, self.n, self.cur = ap, n, 0

            def reset(self):
                self.cur = 0

            def get(self, n, shape=None):
                n_al = (n + 15) // 16 * 16
                assert self.cur + n_al <= self.n, (self.cur, n_al, self.n)
                v = self.ap[:, self.cur:self.cur + n]
                self.cur += n_al
                return v

        AB = Arena(arb, NB)
        AF_ = Arena(arf, NF)

        def PPs(name, lo=0, n=None):
            o, sz = PP_OFF[name]
            if n is None:
                n = sz - lo
            return pp[:, o + lo:o + lo + n]

        def MM(out, lhsT, rhs, start, stop, reads, writes, sig=False, skip=False):
            sch.op(PE, lambda h: h.matmul(out, lhsT=lhsT, rhs=rhs, start=start, stop=stop, skip_group_check=skip),
                   reads, writes, signal=sig)

        def ACTF(out, in_, func, reads, writes, bias=0.0, scale=1.0, eng=None):
            sch.op(ACT, lambda h: h.activation(out=out, in_=in_, func=func, bias=bias, scale=scale), reads, writes)

        def TS(eng, out, in0, s1, s2, op0, op1, reads, writes):
            if op1 is None:
                sch.op(eng, lambda h: h.tensor_scalar(out=out, in0=in0, scalar1=s1, scalar2=None, op0=op0), reads, writes)
            else:
                sch.op(eng, lambda h: h.tensor_scalar(out=out, in0=in0, scalar1=s1, scalar2=s2, op0=op0, op1=op1), reads, writes)

        def STT(eng, out, in0, scalar, in1, op0, op1, reads, writes):
            sch.op(eng, lambda h: h.scalar_tensor_tensor(out=out, in0=in0, scalar=scalar, in1=in1, op0=op0, op1=op1), reads, writes)

        def TT(eng, out, in0, in1, op, reads, writes):
            sch.op(eng, lambda h: h.tensor_tensor(out=out, in0=in0, in1=in1, op=op), reads, writes)

        def CP(eng, out, in_, reads, writes):
            if eng is ACT:
                sch.op(eng, lambda h: h.copy(out=out, in_=in_), reads, writes)
            else:
                sch.op(eng, lambda h: h.tensor_copy(out=out, in_=in_), reads, writes)

        def RECIP(out, in_, reads, writes):
            sch.op(DVE, lambda h: h.reciprocal(out=out, in_=in_), reads, writes)

        def DMA(q, dsem, out, in_, reads, writes):
            sch.dma(q, dsem, lambda h: h.dma_start(out=out, in_=in_), reads, writes)

        ld0 = DSem(sch, "ld0")
        DMA(SP, ld0, pp[:, :], pp_in[:, :], [], [t_const])
        sch.op(POOL, lambda h: h.memset(identf[:, :], 0.0), [], [t_const])
        sch.op(POOL, lambda h: h.affine_select(out=identf[:, :], in_=identf[:, :], pattern=[[-1, 128]], compare_op=ALU.not_equal,
                                               fill=1.0, base=0, channel_multiplier=1), [], [t_const])
        jf = xT[:, 7, 0:128]
        sch.op(POOL, lambda h: h.memset(jf, 0.0), [], [t_const])
        sch.op(POOL, lambda h: h.affine_select(out=jf, in_=jf, pattern=[[1, 128]], compare_op=ALU.not_equal,
                                               fill=1.0, base=-127, channel_multiplier=1), [], [t_const])
        sch.op(POOL, lambda h: h.tensor_copy(out=J_bf[:, :], in_=jf), [], [t_const])
        sch.op(POOL, lambda h: h.memset(ones_bf[:, :], 1.0), [], [t_const])
        sch.op(POOL, lambda h: h.memset(blk_bf[:, :], 0.0), [], [t_const])
        sch.op(POOL, lambda h: h.memset(blk_bf[0:64, 0:64], 1.0), [], [t_const])
        sch.op(POOL, lambda h: h.memset(blk_bf[64:128, 64:128], 1.0), [], [t_const])
        EPSC = dyn[:, 150:151]
        sch.op(POOL, lambda h: h.memset(dyn[:, :], 0.0), [], [t_const])
        sch.op(POOL, lambda h: h.memset(EPSC, EPS), [], [t_const])
        gsub = sb("gsub", [128, DEPTH, 128], F32)
        sch.strict_same_engine = True
        for l in range(DEPTH):
            b = 16 * l
            li = lambda_init_fn(l)
            TS(DVE, dyn[:, b:b + 1], PPs(f"gq_{l}"), 0.125, None, ALU.mult, None, [t_const], [t_const])
            tmp = xT[:, 7, 256:320]
            TT(DVE, tmp, PPs(f"lq1_{l}"), PPs(f"lk1_{l}"), ALU.mult, [t_const], [t_const])
            sch.op(DVE, lambda h, b=b, tmp=tmp: h.reduce_sum(out=dyn[:, b + 2:b + 3], in_=tmp, axis=mybir.AxisListType.X), [t_const], [t_const])
            tmp2 = xT[:, 7, 320:384]
            TT(DVE, tmp2, PPs(f"lq2_{l}"), PPs(f"lk2_{l}"), ALU.mult, [t_const], [t_const])
            sch.op(DVE, lambda h, b=b, tmp2=tmp2: h.reduce_sum(out=dyn[:, b + 3:b + 4], in_=tmp2, axis=mybir.AxisListType.X), [t_const], [t_const])
            ACTF(dyn[:, b + 4:b + 6], dyn[:, b + 2:b + 4], AF.Exp, [t_const], [t_const])
            TT(DVE, dyn[:, b + 1:b + 2], dyn[:, b + 4:b + 5], dyn[:, b + 5:b + 6], ALU.subtract, [t_const], [t_const])
            TS(DVE, dyn[:, b + 1:b + 2], dyn[:, b + 1:b + 2], li, None, ALU.add, None, [t_const], [t_const])
            TS(DVE, gsub[:, l, :], PPs(f"sub_{l}"), 1.0 - li, None, ALU.mult, None, [t_const], [t_const])

        sch.strict_same_engine = False
        ohs = xT[0:32, 6, 0:NV]
        vls = xT[0:NH, 5, 0:NV]
        ebf = xT[0:NH, 4, 0:NV]
        ebb = arb[0:NH, 0:NV]
        t_oh = Tile("oh")
        DMA(SP, ld0, ohs, oh_in[:, :], [], [t_oh])
        DMA(SP, ld0, vls, valid_in[:, :], [], [t_oh])
        rb = PPs("rb")
        for c0 in range(0, NV, 512):
            n = min(512, NV - c0)
            MM(pmisc[0:NH, 0:n], rb[0:32, :], ohs[:, c0:c0 + n], True, True, [t_const, t_oh], [t_misc], sig=True)
            ACTF(ebf[:, c0:c0 + n], pmisc[0:NH, 0:n], AF.Exp, [t_misc], [t_oh])
        TT(DVE, ebb, ebf, vls, ALU.mult, [t_oh], [t_oh])
        t_ebv = Tile("ebv")
        DMA(SP, ld0, ebv_dram[:, :], ebb, [t_oh], [t_ebv])
        sch.barrier([ld0])

        cv_ring = []
        AB.reset()
        for i in range(6):
            cv_ring.append(dict(stg=xT[:, i, :], cv=AB.get(4096), t_s=Tile(f"stg{i}"), t_c=Tile(f"cv{i}"),
                                dl=DSem(sch, f"cvl{i}"), ds=DSem(sch, f"cvs{i}")))
        t_w = Tile("wscratch")
        cvn = [0]
        cast_engs = [DVE, ACT, POOL, DVE, ACT]

        def conv_piece(src_ap, stg_view_fn, cast_in_fn, cast_out_fn, dst_ap, cv_view_fn):
            s = cv_ring[cvn[0] % len(cv_ring)]
            eng = cast_engs[cvn[0] % len(cast_engs)]
            cvn[0] += 1
            DMA(SP, s["dl"], stg_view_fn(s["stg"]), src_ap, [], [s["t_s"]])
            CP(eng, cast_out_fn(s["cv"]), cast_in_fn(s["stg"]), [s["t_s"]], [s["t_c"]])
            DMA(ACT, s["ds"], dst_ap, cv_view_fn(s["cv"]), [s["t_c"]], [])

        for l in range(DEPTH):
            for f in range(2):
                for si, W in enumerate((w_gate[f], w_up[f])):
                    for c0 in range(0, FC, 4):
                        nch = min(4, FC - c0)
                        ncol = nch * 128
                        conv_piece(
                            W[l, :, c0 * 128:c0 * 128 + ncol].rearrange("(k p) n -> p k n", p=128),
                            lambda s, ncol=ncol: s.rearrange("p (k n) -> p k n", k=8)[:, :, 0:ncol],
                            lambda s, ncol=ncol, nch=nch: s.rearrange("p (k n) -> p k n", k=8)[:, :, 0:ncol].rearrange("p k (c j) -> p c k j", c=nch),
                            lambda cvt, nch=nch: cvt[:, 0:nch * 1024].rearrange("p (c k j) -> p c k j", c=nch, k=8),
                            wgu[l, f, c0:c0 + nch, :, si, :, :].rearrange("c p k j -> p c (k j)"),
                            lambda cvt, nch=nch: cvt[:, 0:nch * 1024].rearrange("p (c n) -> p c n", c=nch))
                W = w_down[f]
                for c0 in range(0, FC, 4):
                    nch = min(4, FC - c0)
                    conv_piece(
                        W[l, c0 * 128:(c0 + nch) * 128, :].rearrange("(c p) n -> p c n", p=128),
                        lambda s, nch=nch: s[:, 0:nch * 1024].rearrange("p (c n) -> p c n", c=nch),
                        lambda s, nch=nch: s[:, 0:nch * 1024].rearrange("p (c d j) -> p d c j", c=nch, d=8),
                        lambda cvt, nch=nch: cvt[:, 0:nch * 1024].rearrange("p (d c j) -> p d c j", d=8, c=nch),
                        wdn[l, f, :, :, c0:c0 + nch, :].rearrange("d p c j -> p d (c j)"),
                        lambda cvt, nch=nch: cvt[:, 0:nch * 1024].rearrange("p (d n) -> p d n", d=8))
            for c0 in range(0, 20, 4):
                conv_piece(
                    w_in[l, :, c0 * 128:c0 * 128 + 512].rearrange("(k p) n -> p k n", p=128),
                    lambda s: s.rearrange("p (k n) -> p k n", k=8),
                    lambda s: s.rearrange("p (k c j) -> p c k j", k=8, c=4),
                    lambda cvt: cvt.rearrange("p (c k j) -> p c k j", c=4, k=8),
                    win[l, c0:c0 + 4, :, :, :].rearrange("c p k j -> p c (k j)"),
                    lambda cvt: cvt.rearrange("p (c n) -> p c n", c=4))
            for r0 in range(0, 8, 4):
                conv_piece(
                    w_out[l, r0 * 128:(r0 + 4) * 128, :].rearrange("(r p) n -> p r n", p=128),
                    lambda s: s.rearrange("p (r n) -> p r n", r=4),
                    lambda s: s,
                    lambda cvt: cvt,
                    wob[l, r0 * 128:(r0 + 4) * 128, :].rearrange("(r p) n -> p r n", p=128),
                    lambda cvt: cvt.rearrange("p (r n) -> p r n", r=4))
        sch.barrier([s["ds"] for s in cv_ring] + [s["dl"] for s in cv_ring])

        ldx = DSem(sch, "ldx")
        for k in range(KC):
            DMA(SP, ldx, xT[:, k, :], xT_in[k * 128:(k + 1) * 128, :], [], t_x)
        sch.barrier([ldx])

        st_sems = []

        _dsems = {}

        def new_dsem(name):
            if name not in _dsems:
                _dsems[name] = DSem(sch, name)
                st_sems.append(_dsems[name])
            return _dsems[name]

        def rms_to_hq(hq, t_hq, rstd, t_rstd, xg, tx, gain):
            ACTF(hq, xg, AF.Square, [tx], [t_hq])
            for k in range(KC):
                MM(accb[2][:, :], ones_bf[:, :], hq[:, k, :], k == 0, k == KC - 1, [t_hq, t_const], [t_acc[2]], sig=(k == KC - 1))
            ACTF(rstd, accb[2][:, :], AF.Ln, [t_acc[2], t_const], [t_rstd], bias=EPSC, scale=1.0 / D)
            sch.strict_same_engine = True
            ACTF(rstd, rstd, AF.Exp, [t_rstd], [t_rstd], scale=-0.5)
            sch.strict_same_engine = False
            for k in range(KC):
                STT(DVE, hq[:, k, :], xg[:, k, :], gain[:, k:k + 1], rstd, ALU.mult, ALU.mult, [tx, t_rstd, t_const], [t_hq])

        def phase_ffn(jobs, write_out=False):
            AB.reset(); AF_.reset()
            actT = AB.get(FC * G).rearrange("p (c n) -> p c n", c=FC); t_act = Tile("act")
            hq = AB.get(KC * G).rearrange("p (k n) -> p k n", k=KC); t_hq = Tile("hq")
            gu = [dict(ap=AB.get(2 * KC * 128).rearrange("p (s k j) -> p s k j", s=2, k=KC), t=Tile(f"gu{i}"), d=new_dsem(f"gu{i}")) for i in range(3)]
            wd = [dict(ap=AB.get(FC * 128).rearrange("p (c j) -> p c j", c=FC), t=Tile(f"wd{i}"), d=new_dsem(f"wd{i}")) for i in range(2)]
            rstd = AF_.get(G); t_rstd = Tile("rstd")
            sg = [AF_.get(G), AF_.get(G)]; t_sg = [Tile("sg0"), Tile("sg1")]
            od = new_dsem("outd")
            ngu = 0
            nwd = 0
            for m in range(NG):
                xg = xT[:, :, m * G:(m + 1) * G]
                tx = t_x[m]
                for (l, f) in jobs:
                    gain = PPs(f"g{1 if f == 0 else 2}_{l}")
                    rms_to_hq(hq, t_hq, rstd, t_rstd, xg, tx, gain)
                    for c in range(FC):
                        w = gu[ngu % 3]
                        i2 = ngu % 2
                        ngu += 1
                        DMA(SP, w["d"], w["ap"].rearrange("p s k j -> p (s k j)"), wgu[l, f, c].rearrange("p s k j -> p (s k j)"), [], [w["t"]])
                        ps = psS[i2]
                        for k in range(KC):
                            MM(ps[:, 0:G], w["ap"][:, 0, k, :], hq[:, k, :], k == 0, k == KC - 1, [w["t"], t_hq], [t_S[i2]])
                        for k in range(KC):
                            MM(ps[:, G:2 * G], w["ap"][:, 1, k, :], hq[:, k, :], k == 0, k == KC - 1, [w["t"], t_hq], [t_S[i2]], sig=(k == KC - 1))
                        ACTF(sg[i2], ps[:, 0:G], AF.Silu, [t_S[i2]], [t_sg[i2]])
                        TT(DVE, actT[:, c, :], sg[i2], ps[:, G:2 * G], ALU.mult, [t_sg[i2], t_S[i2]], [t_act])
                    for d in range(KC):
                        w = wd[nwd % 2]
                        i2 = nwd % 2
                        nwd += 1
                        DMA(SP, w["d"], w["ap"].rearrange("p c j -> p (c j)"), wdn[l, f, d].rearrange("p c j -> p (c j)"), [], [w["t"]])
                        for c in range(FC):
                            MM(accb[i2][:, :], w["ap"][:, c, :], actT[:, c, :], c == 0, c == FC - 1, [w["t"], t_act], [t_acc[i2]], sig=(c == FC - 1))
                        STT(DVE, xg[:, d, :], accb[i2][:, :], 0.5, xg[:, d, :], ALU.mult, ALU.add, [t_acc[i2], tx], [tx])
                if write_out:
                    for k in range(KC):
                        DMA(ACT, od, yT[k * 128:(k + 1) * 128, m * G:(m + 1) * G], xT[:, k, m * G:(m + 1) * G], [tx], [])
            return od

        t_u = [Tile(f"u{m}") for m in range(NG)]
        t_q = [Tile(f"q{m}") for m in range(NG)]
        t_agk = [Tile(f"agk{m}") for m in range(NG)]
        t_agv = [Tile(f"agv{m}") for m in range(NG)]
        t_agt = [Tile(f"agt{m}") for m in range(NG)]
        agout_t5 = agout_t.rearrange("(r c p) (m e) -> r c p m e", r=4, p=128, e=32)

        def phase_mixin(l):
            AB.reset(); AF_.reset()
            hq = AB.get(KC * G).rearrange("p (k n) -> p k n", k=KC); t_hq = Tile("hq")
            wi = [dict(ap=AB.get(KC * 128).rearrange("p (k j) -> p k j", k=KC), t=Tile(f"wi{i}"), d=new_dsem(f"wi{i}")) for i in range(4)]
            wv = AB.get(KC * 512).rearrange("p (c k j) -> p c k j", c=4, k=KC); t_wv = Tile("wv"); dwv = new_dsem("wv")
            sqq = AB.get(G); t_sqq = Tile("sqq")
            qst = [dict(ap=AB.get(G), t=Tile(f"qst{i}"), d=new_dsem(f"qst{i}")) for i in range(2)]
            vst = [dict(ap=AB.get(NH * 130).rearrange("p (h e) -> p h e", h=NH), t=Tile(f"vst{i}"), d=new_dsem(f"vst{i}")) for i in range(2)]
            rstd = AF_.get(G); t_rstd = Tile("rstd")
            sgm = AF_.get(G); t_sgm = Tile("sgm")
            ut = [dict(ap=AF_.get(G), t=Tile(f"ut{i}"), d=new_dsem(f"ut{i}")) for i in range(2)]
            rq = AF_.get(G); t_rq = Tile("rq")
            for v in vst:
                sch.op(POOL, lambda h, v=v: h.memset(v["ap"][:, :, 128:129], 1.0), [], [v["t"]])
            DMA(SP, dwv, wv.rearrange("p c k j -> p c (k j)"), win[l, 16:20].rearrange("c p k j -> p c (k j)"), [], [t_wv])
            nwi = 0
            nu = 0
            nq = 0
            nv = 0
            gq_s = dyn[:, 16 * l:16 * l + 1]
            gk = PPs(f"gk_{l}")
            for m in range(NG):
                xg = xT[:, :, m * G:(m + 1) * G]
                tx = t_x[m]
                rms_to_hq(hq, t_hq, rstd, t_rstd, xg, tx, PPs(f"gm_{l}"))

                def proj(c, ps_ap, t_ps, sig_last=True):
                    nonlocal nwi
                    w = wi[nwi % 4]
                    nwi += 1
                    DMA(SP, w["d"], w["ap"].rearrange("p k j -> p (k j)"), win[l, c].rearrange("p k j -> p (k j)"), [], [w["t"]])
                    for k in range(KC):
                        MM(ps_ap, w["ap"][:, k, :], hq[:, k, :], k == 0, k == KC - 1, [w["t"], t_hq], [t_ps], sig=(sig_last and k == KC - 1))

                for cc in range(4):
                    i2 = cc % 2
                    proj(cc, psS[i2][:, 0:G], t_S[i2], sig_last=False)
                    proj(4 + cc, psS[i2][:, G:2 * G], t_S[i2])
                    ACTF(sgm, psS[i2][:, G:2 * G], AF.Sigmoid, [t_S[i2]], [t_sgm])
                    u = ut[nu % 2]
                    nu += 1
                    TT(DVE, u["ap"], psS[i2][:, 0:G], sgm, ALU.mult, [t_S[i2], t_sgm], [u["t"]])
                    DMA(ACT, u["d"], u_dram[cc, :, m * G:(m + 1) * G], u["ap"], [u["t"]], [t_u[m]] if cc == 3 else [])
                    DMA(ACT, u["d"], agin_t[cc * 128:(cc + 1) * 128, m * 32:(m + 1) * 32], u["ap"][:, G - 32:G], [u["t"]], [t_agt[m]] if cc == 3 else [])
                for which in range(2):
                    for hh in range(NH):
                        i2 = hh % 2
                        c = 8 + 4 * which + hh
                        proj(c, psS[i2][:, 0:G], t_S[i2])
                        ACTF(sqq, psS[i2][:, 0:G], AF.Square, [t_S[i2]], [t_sqq])
                        MM(psS[i2][:, G:2 * G], blk_bf[:, :], sqq, True, True, [t_sqq, t_const], [t_S[i2]], sig=True)
                        ACTF(rq, psS[i2][:, G:2 * G], AF.Ln, [t_S[i2], t_const], [t_rq], bias=EPSC, scale=1.0 / 64)
                        sch.strict_same_engine = True
                        ACTF(rq, rq, AF.Exp, [t_rq], [t_rq], scale=-0.5)
                        sch.strict_same_engine = False
                        qs = qst[nq % 2]
                        nq += 1
                        STT(DVE, qs["ap"], psS[i2][:, 0:G], gq_s if which == 0 else gk, rq, ALU.mult, ALU.mult, [t_S[i2], t_rq, t_const], [qs["t"]])
                        if which == 0:
                            DMA(ACT, qs["d"], q_dram[hh, :, m * G:(m + 1) * G], qs["ap"], [qs["t"]], [t_q[m]] if hh == NH - 1 else [])
                        else:
                            DMA(ACT, qs["d"], agin_k[hh][:, m * G:(m + 1) * G], qs["ap"], [qs["t"]], [t_agk[m]] if hh == NH - 1 else [])
                for tb in range(4):
                    i2 = tb % 2
                    for k in range(KC):
                        MM(accb[i2][:, :], hq[:, k, tb * 128:(tb + 1) * 128], wv[:, :, k, :], k == 0, k == KC - 1, [t_hq, t_wv], [t_acc[i2]], sig=(k == KC - 1))
                    vs = vst[nv % 2]
                    nv += 1
                    CP(DVE, vs["ap"][:, :, 0:128], accb[i2][:, :].rearrange("p (h e) -> p h e", h=NH), [t_acc[i2]], [vs["t"]])
                    for hh in range(NH):
                        kbl = 4 * (m % 4) + tb
                        DMA(ACT, vs["d"], agin_v[hh][m // 4][:, kbl * 129:(kbl + 1) * 129], vs["ap"][:, hh, 0:129], [vs["t"]],
                            [t_agv[m]] if (tb == 3 and hh == NH - 1) else [])

        t_gk = [Tile(f"gk{h}") for h in range(NH)]; t_gv = [Tile(f"gv{h}") for h in range(NH)]; t_gt = Tile("gt")
        ccs = DSem(sch, "ccs")
        RG = [[0, 1, 2, 3], [4, 5, 6, 7]]

        def phase_gather():
            sch.barrier(st_sems)
            jobs = [(agin_t, agout_t, t_agt, t_gt)]
            for hh in range(NH):
                jobs.append((agin_k[hh], agout_k[hh], t_agk, t_gk[hh]))
                for hf in range(2):
                    jobs.append((agin_v[hh][hf], agout_v[hh][hf], t_agv, t_gv[hh]))
            for (i_ap, o_ap, tin, tout) in jobs:
                sch.dma(POOL, ccs, lambda h, i_ap=i_ap, o_ap=o_ap: h.collective_compute("AllGather", ALU.bypass, replica_groups=RG,
                                                                                      ins=[i_ap[:, :]], outs=[o_ap[:, :]]),
                        reads=list(tin), writes=[tout], inc=1)
            sch.barrier([])

        def phase_conv(l):
            AB.reset(); AF_.reset()
            woc = AB.get(4 * D).rearrange("p (r n) -> p r n", r=4); t_woc = Tile("woc"); dwoc = new_dsem("woc")
            sqc = AB.get(4 * G).rearrange("p (c n) -> p c n", c=4); t_sqc = Tile("sqc")
            co = AB.get(4 * G).rearrange("p (c n) -> p c n", c=4); t_co = Tile("co")
            dg = AB.get(4 * CONVK * 128).rearrange("p (t j) -> p t j", t=4 * CONVK); t_dg = Tile("dg")
            uwb = [dict(ap=AB.get(G + 32), t=Tile(f"uwb{i}")) for i in range(2)]
            uw = [dict(ap=AF_.get(G + 32), t=Tile(f"uw{i}"), d=new_dsem(f"uw{i}")) for i in range(2)]
            cand = [dict(ap=AF_.get(5 * 32).rearrange("p (r e) -> p r e", r=5), t=Tile(f"cand{i}"), d=new_dsem(f"cand{i}")) for i in range(2)]
            ycv = AF_.get(4 * G).rearrange("p (c n) -> p c n", c=4); t_y = Tile("ycv")
            rc = AF_.get(G); t_rc = Tile("rc")
            DMA(SP, dwoc, woc, wob[l, 0:512, :].rearrange("(r p) n -> p r n", p=128), [], [t_woc])
            coef = PPs("coef")
            cw = PPs(f"cw_{l}")
            cb = PPs(f"cb_{l}")
            cn = PPs(f"cn_{l}")
            for tj in range(4 * CONVK):
                TS(DVE, dg[:, tj, :], identf[:, :], cw[:, tj:tj + 1], None, ALU.mult, None, [t_const], [t_dg])
            n = 0
            for m in range(NG):
                xg = xT[:, :, m * G:(m + 1) * G]
                tx = t_x[m]
                for cc in range(4):
                    w = uw[n % 2]
                    wb = uwb[n % 2]
                    i2 = n % 2
                    cd = cand[n % 2]
                    n += 1
                    eng = DVE
                    DMA(SP, w["d"], w["ap"][:, 32:32 + G], u_dram[cc, :, m * G:(m + 1) * G], [t_u[m]], [w["t"]])
                    DMA(SP, cd["d"], cd["ap"][:, 0:4, :], agout_t5[:, cc, :, m, :].rearrange("r p e -> p r e"), [t_gt], [cd["t"]])
                    if m > 0:
                        DMA(SP, cd["d"], cd["ap"][:, 4, :], agout_t5[3, cc, :, m - 1, :], [t_gt], [cd["t"]])
                    halo = w["ap"][:, 0:32]
                    sch.strict_same_engine = True
                    TS(eng, halo, cd["ap"][:, 0, :], coef[:, 0:1], None, ALU.mult, None, [cd["t"], t_const], [w["t"]])
                    for r in range(1, 5 if m > 0 else 4):
                        STT(eng, halo, cd["ap"][:, r, :], coef[:, r:r + 1], halo, ALU.mult, ALU.add, [cd["t"], t_const], [w["t"]])
                    sch.strict_same_engine = False
                    CP(ACT, wb["ap"], w["ap"], [w["t"]], [wb["t"]])
                    for j in range(CONVK):
                        MM(psS[i2][:, 0:G], dg[:, cc * CONVK + j, :], wb["ap"][:, 2 + j:2 + j + G], j == 0, j == CONVK - 1, [t_dg, wb["t"]], [t_S[i2]], sig=(j == CONVK - 1))
                    TS(DVE, ycv[:, cc, :], psS[i2][:, 0:G], cb[:, cc:cc + 1], None, ALU.add, None, [t_S[i2], t_const], [t_y])
                ACTF(sqc, ycv, AF.Square, [t_y], [t_sqc])
                for cc in range(4):
                    MM(accb[2][:, :], ones_bf[:, :], sqc[:, cc, :], cc == 0, cc == 3, [t_sqc, t_const], [t_acc[2]], sig=(cc == 3))
                ACTF(rc, accb[2][:, :], AF.Ln, [t_acc[2], t_const], [t_rc], bias=EPSC, scale=1.0 / 512)
                sch.strict_same_engine = True
                ACTF(rc, rc, AF.Exp, [t_rc], [t_rc], scale=-0.5)
                sch.strict_same_engine = False
                for cc in range(4):
                    STT(DVE, ycv[:, cc, :], ycv[:, cc, :], cn[:, cc:cc + 1], rc, ALU.mult, ALU.mult, [t_y, t_rc, t_const], [t_y])
                ACTF(co, ycv, AF.Silu, [t_y], [t_co])
                for d in range(KC):
                    i2 = d % 2
                    for r in range(4):
                        MM(accb[i2][:, :], woc[:, r, d * 128:(d + 1) * 128], co[:, r, :], r == 0, r == 3, [t_woc, t_co], [t_acc[i2]], sig=(r == 3))
                    TT(DVE, xg[:, d, :], accb[i2][:, :], xg[:, d, :], ALU.add, [t_acc[i2], tx], [tx])

        def phase_attn(l):
            AB.reset(); AF_.reset()
            E = AB.get(17 * G).rearrange("p (i n) -> p i n", i=17); t_E = Tile("E")
            dE = new_dsem("dE")
            qt = [dict(ap=AB.get(G), t=Tile(f"qt{i}"), d=new_dsem(f"qt{i}")) for i in range(2)]
            kv = [dict(k=AB.get(2 * 2048).rearrange("p (s n) -> p s n", s=2), v=AB.get(16 * 129).rearrange("p (b e) -> p b e", b=16),
                       t=Tile(f"kv{i}"), d=new_dsem(f"kv{i}")) for i in range(2)]
            pt = [dict(ap=AB.get(2 * G), t=Tile(f"pt{i}")) for i in range(3)]
            diffT = AB.get(G); t_dT = Tile("diffT")
            woh = AB.get(D); t_woh = Tile("woh"); dwoh = new_dsem("woh")
            t2q = [AF_.get(128) for _ in range(4)]; t_t2q = [Tile(f"t2{i}") for i in range(4)]
            ddq = [AF_.get(128) for _ in range(4)]; t_ddq = [Tile(f"dd{i}") for i in range(4)]
            dnq = [AF_.get(128) for _ in range(4)]; t_dnq = [Tile(f"dn{i}") for i in range(4)]
            smq = [AF_.get(128) for _ in range(4)]; t_smq = [Tile(f"sm{i}") for i in range(4)]
            for s in kv:
                sch.op(POOL, lambda h, s=s: h.memset(s["k"][64:128, 0, :], 0.0), [], [s["t"]])
                sch.op(POOL, lambda h, s=s: h.memset(s["k"][0:64, 1, :], 0.0), [], [s["t"]])
            lamt = dyn[:, 16 * l + 1:16 * l + 2]
            b31 = PPs("b31")
            nkv = 0
            nqt = 0
            npt = 0
            nS = 0
            nh = 0
            for hh in range(NH):
                for i in range(17):
                    srcE = bass.AP(ebv_dram.tensor, hh * NV + 128 * i, [[1, 128], [1, G]])
                    DMA(SP, dE, E[:, i, :], srcE, [t_ebv], [t_E])
                for i in range(17):
                    MM(pmisc[:, :], J_bf[:, :], E[:, i, :], True, True, [t_E, t_const], [t_misc], sig=True)
                    CP(DVE if i % 2 == 0 else ACT, E[:, i, :], pmisc[:, :], [t_misc], [t_E])
                DMA(SP, dwoh, woh, wob[l, 512 + hh * 128:512 + (hh + 1) * 128, :], [], [t_woh])
                for m in range(NG):
                    xg = xT[:, :, m * G:(m + 1) * G]
                    tx = t_x[m]
                    q = qt[nqt % 2]
                    nqt += 1
                    DMA(SP, q["d"], q["ap"], q_dram[hh, :, m * G:(m + 1) * G], [t_q[m]], [q["t"]])
                    nu = 16 * (m + 1)
                    slots = {}

                    def load_sb(M):
                        nonlocal nkv
                        s = kv[nkv % 2]
                        nkv += 1
                        for r in range(4):
                            for half in range(2):
                                DMA(SP, s["d"], s["k"][half * 64:(half + 1) * 64, half, r * 512:(r + 1) * 512],
                                    agout_k[hh][r * 128 + half * 64:r * 128 + (half + 1) * 64, M * 512:(M + 1) * 512],
                                    [t_gk[hh]], [s["t"]])
                            DMA(SP, s["d"], s["v"][:, 4 * r:4 * r + 4, :].rearrange("p b e -> p (b e)"),
                                agout_v[hh][M // 4][r * 128:(r + 1) * 128, 4 * (M % 4) * 129:(4 * (M % 4) + 4) * 129], [t_gv[hh]], [s["t"]])
                        slots[M] = s

                    def unit(u):
                        M, kb = u // 16, u % 16
                        return slots[M], kb, 16 * (M - m) + kb

                    def QK(u):
                        s, kb, rp = unit(u)
                        i2 = (nS + u) % 2
                        MM(psS[i2][:, 0:G], s["k"][:, 0, kb * 128:(kb + 1) * 128], q["ap"], True, True, [s["t"], q["t"]], [t_S[i2]])
                        MM(psS[i2][:, G:2 * G], s["k"][:, 1, kb * 128:(kb + 1) * 128], q["ap"], True, True, [s["t"], q["t"]], [t_S[i2]], sig=True)

                    def EXPPV(u):
                        s, kb, rp = unit(u)
                        i2 = (nS + u) % 2
                        p = pt[(npt + u) % 3]
                        if rp >= -1:
                            ACTF(p["ap"], psS[i2][:, :], AF.Exp, [t_S[i2]], [p["t"]])
                            for mp in range(2):
                                TT(DVE, p["ap"][:, mp * G:(mp + 1) * G], p["ap"][:, mp * G:(mp + 1) * G], E[:, 15 - rp, :], ALU.mult, [p["t"], t_E], [p["t"]])
                        else:
                            ACTF(p["ap"], psS[i2][:, :], AF.Exp, [t_S[i2], t_const], [p["t"]], bias=b31[:, hh:hh + 1])
                        for qb in range(4):
                            for mp in range(2):
                                a = qb * 2 + mp
                                bk, off = a // 3, (a % 3) * 130
                                MM(accb[bk][:, off:off + 129], p["ap"][:, mp * G + qb * 128:mp * G + (qb + 1) * 128], s["v"][:, kb, 0:129],
                                   (u == 0 and a % 3 == 0), (u == nu - 1), [p["t"], s["t"]], [t_acc[bk]], sig=(a == 7), skip=True)

                    load_sb(0)
                    QK(0)
                    for u in range(nu):
                        if u % 16 == 0 and u // 16 + 1 <= m:
                            load_sb(u // 16 + 1)
                        if u + 1 < nu:
                            QK(u + 1)
                        EXPPV(u)
                    nS += nu
                    npt += nu
                    if stop_after == 6:
                        dacc = dscr("dbg_acc", [128, 3 * 512], F32)
                        dE = dscr("dbg_E", [128, 17 * G], BF16)
                        dpt = dscr("dbg_pt", [128, 3 * 2 * G], BF16)
                        dd_ = new_dsem("dbgd")
                        for bk in range(3):
                            stg = xT[:, 7, bk * 512:(bk + 1) * 512]
                            CP(DVE, stg, accb[bk][:, :], [t_acc[bk]], [t_x[7]])
                            DMA(SP, dd_, dacc[:, bk * 512:(bk + 1) * 512], stg, [t_x[7]], [])
                        DMA(SP, dd_, dE[:, :], E.rearrange("p i n -> p (i n)"), [t_E], [])
                        for i3 in range(3):
                            DMA(SP, dd_, dpt[:, i3 * 2 * G:(i3 + 1) * 2 * G], pt[i3]["ap"], [pt[i3]["t"]], [])
                    sch.strict_same_engine = True
                    QB = range(4)
                    A1 = [accb[(2 * qb) // 3][:, ((2 * qb) % 3) * 130:((2 * qb) % 3) * 130 + 129] for qb in QB]
                    A2 = [accb[(2 * qb + 1) // 3][:, ((2 * qb + 1) % 3) * 130:((2 * qb + 1) % 3) * 130 + 129] for qb in QB]
                    tA1 = [t_acc[(2 * qb) // 3] for qb in QB]
                    tA2 = [t_acc[(2 * qb + 1) // 3] for qb in QB]
                    for qb in QB:
                        RECIP(smq[qb][:, 0:1], A1[qb][:, 128:129], [tA1[qb]], [t_smq[qb]])
                    for qb in QB:
                        RECIP(smq[qb][:, 1:2], A2[qb][:, 128:129], [tA2[qb]], [t_smq[qb]])
                    for qb in QB:
                        TS(DVE, smq[qb][:, 1:2], smq[qb][:, 1:2], lamt, None, ALU.mult, None, [t_smq[qb], t_const], [t_smq[qb]])
                    for qb in QB:
                        TS(DVE, t2q[qb], A2[qb][:, 0:128], smq[qb][:, 1:2], None, ALU.mult, None, [tA2[qb], t_smq[qb]], [t_t2q[qb]])
                    for qb in QB:
                        STT(DVE, ddq[qb], A1[qb][:, 0:128], smq[qb][:, 0:1], t2q[qb], ALU.mult, ALU.subtract, [tA1[qb], t_smq[qb], t_t2q[qb]], [t_ddq[qb]])
                    for qb in QB:
                        TT(DVE, t2q[qb], ddq[qb], ddq[qb], ALU.mult, [t_ddq[qb]], [t_t2q[qb]])
                    for qb in QB:
                        sch.op(DVE, lambda h, qb=qb: h.reduce_sum(out=smq[qb][:, 2:3], in_=t2q[qb], axis=mybir.AxisListType.X), [t_t2q[qb]], [t_smq[qb]])
                    for qb in QB:
                        ACTF(smq[qb][:, 3:4], smq[qb][:, 2:3], AF.Ln, [t_smq[qb], t_const], [t_smq[qb]], bias=EPSC, scale=1.0 / 128)
                    for qb in QB:
                        ACTF(smq[qb][:, 3:4], smq[qb][:, 3:4], AF.Exp, [t_smq[qb]], [t_smq[qb]], scale=-0.5)
                    for qb in QB:
                        STT(DVE, dnq[qb], ddq[qb], smq[qb][:, 3:4], gsub[:, l, :], ALU.mult, ALU.mult, [t_ddq[qb], t_smq[qb], t_const], [t_dnq[qb]])
                    for hf in range(2):
                        for q2 in range(2):
                            qb = 2 * hf + q2
                            sch.op(PE, lambda h, qb=qb, q2=q2: h.transpose(pmisc[:, q2 * 256:q2 * 256 + 128], dnq[qb], identf[:, :]), [t_dnq[qb], t_const], [t_misc])
                        CP(ACT, diffT[:, hf * 256:(hf + 1) * 256].rearrange("p (a b) -> p a b", a=2),
                           pmisc[:, :].rearrange("p (a b) -> p a b", a=2)[:, :, 0:128], [t_misc], [t_dT])
                    sch.strict_same_engine = False
                    for d in range(KC):
                        i3 = d % 3
                        MM(accb[i3][:, :], woh[:, d * 128:(d + 1) * 128], diffT, True, True, [t_woh, t_dT], [t_acc[i3]], sig=True)
                        TT(DVE, xg[:, d, :], accb[i3][:, :], xg[:, d, :], ALU.add, [t_acc[i3], tx], [tx])
                    if stop_after == 6:
                        ddT = dscr("dbg_dT", [128, G], BF16)
                        DMA(SP, new_dsem("dbgd"), ddT[:, :], diffT, [t_dT], [])
                        return

        def dump_x():
            od = new_dsem("outd")
            for m in range(NG):
                for k in range(KC):
                    DMA(ACT, od, yT[k * 128:(k + 1) * 128, m * G:(m + 1) * G], xT[:, k, m * G:(m + 1) * G], [t_x[m]], [])

        def run_all():
            if stop_after == 0:
                return False
            for l in range(DEPTH):
                jobs = [(l, 0)] if l == 0 else [(l - 1, 1), (l, 0)]
                phase_ffn(jobs)
                sch.barrier(st_sems)
                if stop_after == 1:
                    return False
                phase_mixin(l)
                if stop_after == 2:
                    sch.barrier(st_sems)
                    return False
                phase_gather()
                if debug:
                    dk = dscr("dbg_k", [4 * 128, TOK], BF16)
                    dv = dscr("dbg_v", [4 * 128, 16 * 129], BF16)
                    dt_ = dscr("dbg_t", [4 * 4 * 128, NG * 32], F32)
                    dd_ = new_dsem("dbgd")
                    DMA(SP, dd_, dk[:, :], agout_k[1][:, :], [t_gk[1]], [])
                    DMA(SP, dd_, dv[:, :], agout_v[1][1][:, :], [t_gv[1]], [])
                    DMA(SP, dd_, dt_[:, :], agout_t[:, :], [t_gt], [])
                if stop_after == 3:
                    sch.barrier(st_sems)
                    return False
                phase_conv(l)
                sch.barrier(st_sems)
                if stop_after == 4:
                    return False
                phase_attn(l)
                sch.barrier(st_sems)
                if stop_after in (5, 6):
                    return False
            return True

        if run_all():
            phase_ffn([(DEPTH - 1, 1)], write_out=True)
        else:
            dump_x()
        sch.barrier(st_sems + [ccs])
        sch.replay()
    return nc


def _rel_bucket_np(n):
    n = np.asarray(n, dtype=np.int64)
    nf = np.maximum(n, 1).astype(np.float64)
    large = 16 + np.floor(np.log(nf / 16.0) / math.log(128 / 16) * 16 + 1e-9).astype(np.int64)
    large = np.minimum(large, 31)
    return np.where(n < 16, n, large)


def _core_consts(j):
    npr = np.arange(NV)
    dist = npr + 512 * j - 2047
    ok = dist >= 0
    bk = _rel_bucket_np(np.maximum(dist, 0))
    oh = np.zeros((32, NV), np.float32)
    oh[bk[ok], npr[ok]] = 1.0
    valid = np.broadcast_to(ok.astype(np.float32)[None, :], (NH, NV)).copy()
    coef = np.zeros(5, np.float32)
    if j >= 1:
        coef[j - 1] = 1.0
    else:
        coef[4] = 1.0
    return oh, valid, coef


def _pack_params(inp, coef):
    pp = np.zeros((128, PP_N), np.float32)

    def put(name, arr):
        o, n = PP_OFF[name]
        arr = np.asarray(arr, np.float32)
        assert arr.shape == (128, n), (name, arr.shape, n)
        pp[:, o:o + n] = arr

    f = lambda a: np.asarray(a, np.float32)
    for l in range(DEPTH):
        put(f"g1_{l}", f(inp["ffn1_norm"][l]).reshape(KC, 128).T)
        put(f"gm_{l}", f(inp["mix_norm"][l]).reshape(KC, 128).T)
        put(f"g2_{l}", f(inp["ffn2_norm"][l]).reshape(KC, 128).T)
        cw = f(inp["conv_w"][l])
        put(f"cw_{l}", cw.reshape(CONVK, 4, 128).transpose(2, 1, 0).reshape(128, 4 * CONVK))
        put(f"cb_{l}", f(inp["conv_b"][l]).reshape(4, 128).T)
        put(f"cn_{l}", f(inp["conv_norm"][l]).reshape(4, 128).T)
        put(f"gq_{l}", np.tile(f(inp["q_norm"][l]), 2)[:, None])
        put(f"gk_{l}", np.tile(f(inp["k_norm"][l]), 2)[:, None])
        for nm, key in (("lq1", "lambda_q1"), ("lk1", "lambda_k1"), ("lq2", "lambda_q2"), ("lk2", "lambda_k2")):
            put(f"{nm}_{l}", np.broadcast_to(f(inp[key][l])[None, :], (128, 64)))
        put(f"sub_{l}", np.broadcast_to(f(inp["subln_norm"][l])[None, :], (128, 128)))
    rbias = f(inp["rel_bias"])
    put("b31", np.broadcast_to(rbias[31][None, :], (128, NH)))
    rbp = np.zeros((128, NH), np.float32)
    rbp[0:32] = rbias
    put("rb", rbp)
    put("coef", np.broadcast_to(coef[None, :], (128, 5)))
    return pp


_NC_CACHE = {}


def kernel(**inputs):
    inp = {k: np.asarray(v) for k, v in inputs.items()}
    x = np.asarray(inp["x"], np.float32)
    if "nc" not in _NC_CACHE:
        _NC_CACHE["nc"] = build_program()
    nc = _NC_CACHE["nc"]
    in_maps = []
    wnames = ["ffn1_w_gate", "ffn2_w_gate", "ffn1_w_up", "ffn2_w_up", "ffn1_w_down", "ffn2_w_down", "w_in", "w_out"]
    wts = {k: np.ascontiguousarray(inp[k], dtype=np.float32) for k in wnames}
    for c in range(NCORES):
        b, j = c // 4, c % 4
        xs = x[b].reshape(NG, 4, G, D)[:, j].reshape(TOK, D)
        oh, valid, coef = _core_consts(j)
        d = dict(wts)
        d["xT"] = np.ascontiguousarray(xs.T)
        d["pp"] = _pack_params(inp, coef)
        d["oh"] = oh
        d["valid"] = valid
        in_maps.append(d)
    res = run_bass_kernel_spmd(nc, in_maps, core_ids=list(range(NCORES)))
    out = np.empty((B, S, D), np.float32)
    for c in range(NCORES):
        b, j = c // 4, c % 4
        y = np.asarray(res.results[c]["yT"], np.float32).T.reshape(NG, G, D)
        out[b].reshape(NG, 4, G, D)[:, j] = y
    return out
```
